# Optimizing a Trainium2 kernel written in Bass

```python
import jax, jax.numpy as jnp
from jax import lax
import numpy as np

D_MODEL = 2048
BATCH = 2
SEQ = 4096
DEPTH = 1

HEAD_DIM = 128
N_FOX = 6
N_SB = 6
N_MEM = 4
N_MEM_TOK = 256
D_FF = 5632
CONV_W = 3
BLOCK_Q = 128
N_BRANCH = 3
EPS = 1e-6

FOX_W = N_FOX * HEAD_DIM
SB_W = N_SB * HEAD_DIM
MEM_W = N_MEM * HEAD_DIM
IN_COLS = 3 * FOX_W + N_FOX + 3 * SB_W + MEM_W + N_BRANCH * D_MODEL

kernel_name = 'fox_stickbreak_memory_gated_hybrid'


def rms_norm(t, g):
    tf = t.astype(jnp.float32)
    y = tf * lax.rsqrt(jnp.mean(tf * tf, axis=-1, keepdims=True) + EPS)
    return (y * g.astype(jnp.float32)).astype(t.dtype)


def split_heads(t, n):
    b, s, _ = t.shape
    return t.reshape(b, s, n, HEAD_DIM).transpose(0, 2, 1, 3)


def merge_heads(t):
    b, h, s, d = t.shape
    return t.transpose(0, 2, 1, 3).reshape(b, s, h * d)


def to_blocks(t):
    b, h, s = t.shape[:3]
    nb = s // BLOCK_Q
    t = t.reshape((b, h, nb, BLOCK_Q) + t.shape[3:])
    return jnp.moveaxis(t, 2, 0)


def from_blocks(t):
    nb, b, h, q, d = t.shape
    return jnp.moveaxis(t, 0, 2).reshape(b, h, nb * q, d)


def forgetting_attention(q, k, v, log_f):
    s_len = q.shape[2]
    scale = HEAD_DIM ** -0.5
    c = jnp.cumsum(log_f, axis=-1)
    key_pos = jnp.arange(s_len)

    def block(args):
        q_blk, c_blk, i = args
        q_pos = i * BLOCK_Q + jnp.arange(BLOCK_Q)
        s = jnp.einsum('bhqd,bhkd->bhqk', q_blk, k).astype(jnp.float32) * scale
        s = s + c_blk[..., :, None] - c[..., None, :]
        s = jnp.where(key_pos[None, :] <= q_pos[:, None], s, -jnp.inf)
        p = jax.nn.softmax(s, axis=-1)
        return jnp.einsum('bhqk,bhkd->bhqd', p.astype(v.dtype), v)

    nb = s_len // BLOCK_Q
    out = lax.map(block, (to_blocks(q), to_blocks(c), jnp.arange(nb)))
    return from_blocks(out)


def stick_breaking_attention(q, k, v):
    s_len = q.shape[2]
    scale = HEAD_DIM ** -0.5
    key_pos = jnp.arange(s_len)

    def block(args):
        q_blk, i = args
        q_pos = i * BLOCK_Q + jnp.arange(BLOCK_Q)
        z = jnp.einsum('bhqd,bhkd->bhqk', q_blk, k).astype(jnp.float32) * scale
        before = key_pos[None, :] < q_pos[:, None]
        log_beta = jax.nn.log_sigmoid(z)
        log_1m = jnp.where(before, log_beta - z, 0.0)
        log_remain = lax.cumsum(log_1m, axis=log_1m.ndim - 1, reverse=True) - log_1m
        a = jnp.where(before, jnp.exp(log_beta + log_remain), 0.0)
        return jnp.einsum('bhqk,bhkd->bhqd', a.astype(v.dtype), v)

    nb = s_len // BLOCK_Q
    out = lax.map(block, (to_blocks(q), jnp.arange(nb)))
    return from_blocks(out)


def memory_attention(q, k, v):
    s = jnp.einsum('bhqd,bhmd->bhqm', q, k).astype(jnp.float32) * (HEAD_DIM ** -0.5)
    p = jax.nn.softmax(s, axis=-1)
    return jnp.einsum('bhqm,bhmd->bhqd', p.astype(v.dtype), v)


def causal_depthwise_conv(u, w, b):
    c = u.shape[-1]
    y = lax.conv_general_dilated(u, w[:, None, :], window_strides=(1,),
                                 padding=[(CONV_W - 1, 0)],
                                 dimension_numbers=('NWC', 'WIO', 'NWC'),
                                 feature_group_count=c)
    return y + b


def setup_inputs(seed: int = 0) -> dict:
    key = jax.random.key(seed)
    ks = jax.random.split(key, 24)
    f32 = jnp.float32
    nrm = lambda k, shape, fan_in: jax.random.normal(k, shape, f32) * (fan_in ** -0.5)
    gain = lambda k, shape: 1.0 + 0.05 * jax.random.normal(k, shape, f32)
    x = jax.random.normal(ks[0], (BATCH, SEQ, D_MODEL), f32)
    mem = jax.random.normal(ks[1], (BATCH, N_MEM_TOK, D_MODEL), f32)
    b_forget = (jnp.linspace(1.0, 6.0, N_FOX, dtype=f32)[None, :]
                + 0.1 * jax.random.normal(ks[4], (DEPTH, N_FOX), f32))
    return {
        'x': x,
        'mem': mem,
        'g_mix': gain(ks[2], (DEPTH, D_MODEL)),
        'w_in': nrm(ks[3], (DEPTH, D_MODEL, IN_COLS), D_MODEL),
        'b_forget': b_forget,
        'g_q_fox': gain(ks[5], (DEPTH, HEAD_DIM)),
        'g_k_fox': gain(ks[6], (DEPTH, HEAD_DIM)),
        'g_mem': gain(ks[7], (DEPTH, D_MODEL)),
        'w_mem_kv': nrm(ks[8], (DEPTH, D_MODEL, 2 * MEM_W), D_MODEL),
        'g_q_mem': gain(ks[9], (DEPTH, HEAD_DIM)),
        'g_k_mem': gain(ks[10], (DEPTH, HEAD_DIM)),
        'w_br_fox': nrm(ks[11], (DEPTH, FOX_W, D_MODEL), FOX_W),
        'w_br_sb': nrm(ks[12], (DEPTH, SB_W, D_MODEL), SB_W),
        'w_br_mem': nrm(ks[13], (DEPTH, MEM_W, D_MODEL), MEM_W),
        'b_gate': 0.01 * jax.random.normal(ks[14], (DEPTH, N_BRANCH, D_MODEL), f32),
        'w_out': nrm(ks[15], (DEPTH, D_MODEL, D_MODEL), D_MODEL),
        'g_ffn': gain(ks[16], (DEPTH, D_MODEL)),
        'w_up': nrm(ks[17], (DEPTH, D_MODEL, 2 * D_FF), D_MODEL),
        'conv_w': nrm(ks[18], (DEPTH, CONV_W, 2 * D_FF), CONV_W),
        'conv_b': 0.01 * jax.random.normal(ks[19], (DEPTH, 2 * D_FF), f32),
        'w_down': nrm(ks[20], (DEPTH, D_FF, D_MODEL), D_FF),
    }


def reference(x, mem, g_mix, w_in, b_forget, g_q_fox, g_k_fox, g_mem, w_mem_kv,
              g_q_mem, g_k_mem, w_br_fox, w_br_sb, w_br_mem, b_gate, w_out,
              g_ffn, w_up, conv_w, conv_b, w_down):
    b, s, _ = x.shape
    cuts = np.cumsum([FOX_W, FOX_W, FOX_W, N_FOX, SB_W, SB_W, SB_W, MEM_W]).tolist()
    for l in range(DEPTH):
        h = rms_norm(x, g_mix[l])
        proj = h @ w_in[l]
        fq, fk, fv, f_logit, sq, sk, sv, mq, gates = jnp.split(proj, cuts, axis=-1)

        qa = rms_norm(split_heads(fq, N_FOX), g_q_fox[l])
        ka = rms_norm(split_heads(fk, N_FOX), g_k_fox[l])
        va = split_heads(fv, N_FOX)
        log_f = jax.nn.log_sigmoid(f_logit.astype(jnp.float32) + b_forget[l].astype(jnp.float32))
        o_fox = merge_heads(forgetting_attention(qa, ka, va, log_f.transpose(0, 2, 1)))

        o_sb = merge_heads(stick_breaking_attention(split_heads(sq, N_SB), split_heads(sk, N_SB),
                                                    split_heads(sv, N_SB)))

        mkv = rms_norm(mem, g_mem[l]) @ w_mem_kv[l]
        mk, mv = jnp.split(mkv, 2, axis=-1)
        qm = rms_norm(split_heads(mq, N_MEM), g_q_mem[l])
        km = rms_norm(split_heads(mk, N_MEM), g_k_mem[l])
        o_mem = merge_heads(memory_attention(qm, km, split_heads(mv, N_MEM)))

        g = jax.nn.sigmoid(gates.reshape(b, s, N_BRANCH, D_MODEL) + b_gate[l])
        merged = (g[:, :, 0] * (o_fox @ w_br_fox[l])
                  + g[:, :, 1] * (o_sb @ w_br_sb[l])
                  + g[:, :, 2] * (o_mem @ w_br_mem[l]))
        x = x + merged @ w_out[l]

        h2 = rms_norm(x, g_ffn[l])
        u = causal_depthwise_conv(h2 @ w_up[l], conv_w[l], conv_b[l])
        u_gate, u_val = jnp.split(u, 2, axis=-1)
        x = x + (jax.nn.silu(u_gate) * u_val) @ w_down[l]
    return x
```

```python
import numpy as np
import ml_dtypes
from contextlib import ExitStack
import concourse.bass as bass
import concourse.mybir as mybir
from concourse.bass_utils import run_bass_kernel_spmd

F32 = mybir.dt.float32
BF16 = mybir.dt.bfloat16
AF = mybir.ActivationFunctionType
ALU = mybir.AluOpType

D = 2048
KC = 16
S = 4096
NB = 32
NQ = 1040
COLT = [(0, 512), (512, 1024), (1024, 1040)]
DFF = 5632
NCH = 44
EPS = 1e-6
SCALE = 128 ** -0.5
NEG = -30000.0
IN_COLS = 11270
C_FQ, C_FK, C_FV, C_FL, C_SQ, C_SK, C_SV, C_MQ, C_G = 0, 768, 1536, 2304, 2310, 3078, 3846, 4614, 5126

V_GMIX, V_GFFN, V_GMEM, V_GQF, V_GKF, V_GQM, V_GKM = 0, 16, 32, 48, 49, 50, 51
V_BG, V_CW, V_CB, V_BF, V_SEL, V_HV, NV = 52, 100, 364, 452, 458, 462, 464
MB_ONES, MB_NTI, MB_Z, MB_SM, MB_SH, NMB = 0, 128, 256, 384, 896, 1408


class Buf:
    __slots__ = ("w", "r")

    def __init__(self):
        self.w = None
        self.r = {}


class Eng:
    def __init__(self, h, sem):
        self.h = h
        self.sem = sem
        self.cnt = 0
        self.waited = {}

    def wait(self, tok):
        if tok is None:
            return
        sem, val = tok
        k = id(sem)
        if self.waited.get(k, 0) >= val:
            return
        self.h.wait_ge(sem, val)
        self.waited[k] = val


class DSem:
    def __init__(self, sem):
        self.sem = sem
        self.val = 0


class KB:
    def __init__(self, nc, es):
        self.nc = nc
        self.es = es
        mk = lambda n: es.enter_context(nc.semaphore(n))
        self.pe = Eng(nc.tensor, mk("s_pe"))
        self.act = Eng(nc.scalar, mk("s_act"))
        self.dve = Eng(nc.vector, mk("s_dve"))
        self.pool = Eng(nc.gpsimd, mk("s_pool"))
        self.sp = Eng(nc.sync, mk("s_sp"))
        self.engs = [self.pe, self.act, self.dve, self.pool, self.sp]
        self.dsems = []
        self.nsem = 0

    def dsem(self):
        self.nsem += 1
        d = DSem(self.es.enter_context(self.nc.semaphore("ds%d" % self.nsem)))
        self.dsems.append(d)
        return d

    def _pre(self, e, reads, writes):
        for b in reads:
            if b.w is not None and not (e is self.pe and b.w[0] is e.sem):
                e.wait(b.w)
        for b in writes:
            for tok in b.r.values():
                if tok[0] is not e.sem:
                    e.wait(tok)
            if b.w is not None and b.w[0] is not e.sem:
                e.wait(b.w)

    def _post(self, tok, reads, writes):
        for b in reads:
            k = id(tok[0])
            if k not in b.r or b.r[k][1] < tok[1]:
                b.r[k] = tok
        for b in writes:
            b.w = tok
            b.r = {}

    def op(self, e, emit, reads=(), writes=(), inc=True):
        self._pre(e, reads, writes)
        ins = emit()
        if inc:
            e.cnt += 1
            ins.then_inc(e.sem, 1)
            tok = (e.sem, e.cnt)
        else:
            tok = (e.sem, e.cnt + 1)
        self._post(tok, reads, writes)

    def dma(self, q, out, in_, ds, reads=(), writes=()):
        for b in writes:
            if b.w is not None and b.w[0] is ds.sem:
                b.w = None
        self._pre(q, reads, writes)
        ds.val += 16
        q.h.dma_start(out=out, in_=in_).then_inc(ds.sem, 16)
        self._post((ds.sem, ds.val), reads, writes)

    def barrier(self):
        for e in self.engs:
            for o in self.engs:
                if o is not e and o.cnt > 0:
                    e.wait((o.sem, o.cnt))
            for d in self.dsems:
                if d.val > 0:
                    e.wait((d.sem, d.val))

    def mm(self, out, lhsT, rhs, start, stop, reads, writes, inc=None, skip=False):
        if inc is None:
            inc = stop
        if skip:
            self.op(self.pe, lambda: self.nc.tensor.matmul(out, lhsT, rhs, start=start, stop=stop, skip_group_check=True),
                    reads, writes, inc)
        else:
            self.op(self.pe, lambda: self.nc.tensor.matmul(out, lhsT, rhs, start=start, stop=stop), reads, writes, inc)

    def actf(self, out, in_, func, reads, writes, bias=0.0, scale=1.0):
        self.op(self.act, lambda: self.nc.scalar.activation(out=out, in_=in_, func=func, bias=bias, scale=scale), reads, writes)

    def tt(self, e, out, in0, in1, op, reads, writes):
        self.op(e, lambda: e.h.tensor_tensor(out=out, in0=in0, in1=in1, op=op), reads, writes)

    def ts(self, e, out, in0, s1, s2, op0, op1, reads, writes):
        if s2 is None:
            self.op(e, lambda: e.h.tensor_scalar(out=out, in0=in0, scalar1=s1, scalar2=None, op0=op0), reads, writes)
        else:
            self.op(e, lambda: e.h.tensor_scalar(out=out, in0=in0, scalar1=s1, scalar2=s2, op0=op0, op1=op1), reads, writes)

    def stt(self, e, out, in0, scalar, in1, op0, op1, reads, writes):
        self.op(e, lambda: e.h.scalar_tensor_tensor(out=out, in0=in0, scalar=scalar, in1=in1, op0=op0, op1=op1), reads, writes)

    def cp(self, e, out, in_, reads, writes):
        self.op(e, lambda: e.h.tensor_copy(out=out, in_=in_), reads, writes)

    def mset(self, e, ap, val, writes):
        self.op(e, lambda: e.h.memset(ap, val), (), writes)


class T:
    _n = [0]

    def __init__(self, es, nc, name, shape, dt, psum=False):
        T._n[0] += 1
        name = "t%d_%s" % (T._n[0], name)
        if psum:
            self.t = es.enter_context(nc.psum_tensor(name, shape, dt))
        else:
            self.t = es.enter_context(nc.sbuf_tensor(name, shape, dt))
        self.b = Buf()

    def __getitem__(self, idx):
        return self.t[idx]


def build_program():
    nc = bass.Bass("TRN2", target_bir_lowering=False)
    dr = lambda n, s, dt, kind="ExternalInput": nc.dram_tensor(n, list(s), dt, kind=kind).ap()
    xo_d = dr("xo", (D, NQ), F32)
    xa_d = dr("xa", (D, S), F32)
    mt_d = dr("mt", (D, 256), F32)
    w_in_d = dr("w_in", (D, IN_COLS), F32)
    w_mkv_d = dr("w_mkv", (D, 1024), F32)
    w_br_d = dr("w_br", (D, D), F32)
    w_out_d = dr("w_out", (D, D), F32)
    w_up_d = dr("w_up", (D, 2 * DFF), F32)
    w_dn_d = dr("w_dn", (DFF, D), F32)
    vec_d = dr("vec", (128, NV), F32)
    cf_d = dr("cf", (128, 384), F32)
    mf_d = dr("mf", (128, 1024), F32)
    mb_d = dr("mb", (128, NMB), BF16)
    out_d = dr("outT", (D, 1024), F32, kind="ExternalOutput")
    kts_d = dr("kts", (12, 128, S), BF16, kind="Internal")
    vs_d = dr("vs", (2, NB, 128, 768), BF16, kind="Internal")
    xm_d = dr("xmid", (KC, 128, NQ), F32, kind="Internal")

    w_in_v = w_in_d.rearrange("(k p) c -> p k c", p=128)
    xa_v = xa_d.rearrange("(k p) c -> p k c", p=128)
    xo_v = xo_d.rearrange("(k p) c -> p k c", p=128)

    es = ExitStack()
    with es:
        kb = KB(nc, es)
        pe, act, dve, pool, sp = kb.pe, kb.act, kb.dve, kb.pool, kb.sp
        vec = T(es, nc, "vec", [128, NV], F32)
        cf = T(es, nc, "cf", [128, 384], F32)
        mf = T(es, nc, "mf", [128, 1024], F32)
        mb = T(es, nc, "mb", [128, NMB], BF16)
        gsc = T(es, nc, "gsc", [128, 2], F32)
        ctok = T(es, nc, "ctok", [128, NB, 6], F32)
        cown = T(es, nc, "cown", [128, 8, 6], F32)
        chal = T(es, nc, "chal", [128, 8, 6], F32)
        ps = [T(es, nc, "ps%d" % i, [128, 512], F32, psum=True) for i in range(8)]
        for tl, dd in ((vec, vec_d), (cf, cf_d), (mf, mf_d), (mb, mb_d)):
            kb.dma(sp, tl[:, :], dd, kb.dsem(), (), (tl.b,))
        ident = cf[:, 0:128]
        tri = cf[:, 128:256]
        ones_f = cf[:, 256:384]
        ones_b = mb[:, MB_ONES:MB_ONES + 128]
        nti_b = mb[:, MB_NTI:MB_NTI + 128]
        zer_b = mb[:, MB_Z:MB_Z + 128]
        vcol = lambda i: vec[:, i:i + 1]
        kb.ts(dve, gsc[:, 0:1], vcol(V_GQF), SCALE, None, ALU.mult, None, (vec.b,), (gsc.b,))
        kb.ts(dve, gsc[:, 1:2], vcol(V_GQM), SCALE, None, ALU.mult, None, (vec.b,), (gsc.b,))

        rs2 = T(es, nc, "rs2", [128, NQ], F32)
        psrr = [0]
        psrr2 = [0]

        def psnext():
            t = ps[psrr[0] % 8]
            psrr[0] += 1
            return t

        def psrot():
            t = ps[2 + psrr2[0] % 6]
            psrr2[0] += 1
            return t

        def rms_rstd(sq_src_fn, nchunks, ncols, pbank, rstd_t, sqts, n_feat, alt=True):
            for k in range(nchunks):
                src, srcb = sq_src_fn(k)
                sq = sqts[k % len(sqts)]
                e = dve if (k % 2 == 0 or not alt) else pool
                kb.tt(e, sq[:, :ncols], src, src, ALU.mult, srcb, (sq.b,))
                kb.mm(pbank[:, :ncols], ones_b, sq[:, :ncols], k == 0, k == nchunks - 1, (sq.b, mb.b), (pbank.b,), inc=True)
            kb.actf(rstd_t[:, :ncols], pbank[:, :ncols], AF.Ln, (pbank.b,), (rstd_t.b,), bias=EPS, scale=1.0 / n_feat)
            kb.actf(rstd_t[:, :ncols], rstd_t[:, :ncols], AF.Exp, (rstd_t.b,), (rstd_t.b,), scale=-0.5)

        with ExitStack() as pes:
            wK = T(pes, nc, "wK", [128, KC, 768], BF16)
            wV = T(pes, nc, "wV", [128, KC, 774], BF16)
            xts = [T(pes, nc, "xt%d" % i, [128, KC, 512], F32) for i in range(2)]
            hts = [T(pes, nc, "ht%d" % i, [128, KC, 512], BF16) for i in range(2)]
            sqts = [T(pes, nc, "sq%d" % i, [128, 512], BF16) for i in range(4)]
            rstd = T(pes, nc, "rstd", [128, 512], F32)
            kf = [T(pes, nc, "kf%d" % i, [128, 512], F32) for i in range(2)]
            krs = [T(pes, nc, "krs%d" % i, [128, 512], F32) for i in range(2)]
            kst = [T(pes, nc, "kst%d" % i, [128, 512], BF16) for i in range(3)]
            vst = [T(pes, nc, "vst%d" % i, [128, 768], BF16) for i in range(2)]
            lfp = T(pes, nc, "lfp", [128, NB, 6], F32)
            ltmp = T(pes, nc, "ltmp", [128, 8], F32)
            ds_wk, ds_wv = kb.dsem(), kb.dsem()
            ds_x = [kb.dsem(), kb.dsem()]
            ds_ks = [kb.dsem() for _ in range(3)]
            ds_vs = [kb.dsem() for _ in range(2)]
            kcnt = [0]
            vcnt = [0]
            for pas in range(2):
                ck = C_FK if pas == 0 else C_SK
                cv = C_FV if pas == 0 else C_SV
                nv = 774 if pas == 0 else 768
                for q4 in range(4):
                    kb.dma(pool, wK[:, q4 * 4:(q4 + 1) * 4, :], w_in_v[:, q4 * 4:(q4 + 1) * 4, ck:ck + 768], ds_wk, (), (wK.b,))
                for q4 in range(4):
                    kb.dma(pool, wV[:, q4 * 4:(q4 + 1) * 4, 0:nv], w_in_v[:, q4 * 4:(q4 + 1) * 4, cv:cv + nv], ds_wv, (), (wV.b,))

                def load_x(tt_):
                    xt = xts[tt_ % 2]
                    for q4 in range(4):
                        kb.dma(sp, xt[:, q4 * 4:(q4 + 1) * 4, :], xa_v[:, q4 * 4:(q4 + 1) * 4, tt_ * 512:(tt_ + 1) * 512],
                               ds_x[tt_ % 2], (), (xt.b,))

                load_x(0)
                for tt_ in range(8):
                    if tt_ + 1 < 8:
                        load_x(tt_ + 1)
                    xt = xts[tt_ % 2]
                    ht = hts[tt_ % 2]
                    pb = psnext()
                    rms_rstd(lambda k: (xt[:, k, :], (xt.b,)), KC, 512, pb, rstd, sqts, D)
                    for k in range(KC):
                        e = dve
                        kb.stt(e, ht[:, k, :], xt[:, k, :], vcol(V_GMIX + k), rstd[:, :], ALU.mult, ALU.mult,
                               (xt.b, vec.b, rstd.b), (ht.b,))
                    for h in range(6):
                        acc = psnext()
                        for k in range(KC):
                            kb.mm(acc[:, :], wK[:, k, h * 128:(h + 1) * 128], ht[:, k, :], k == 0, k == KC - 1,
                                  (wK.b, ht.b), (acc.b,))
                        st = kst[kcnt[0] % 3]
                        dss = ds_ks[kcnt[0] % 3]
                        kcnt[0] += 1
                        if pas == 0:
                            kfi = kf[h % 2]
                            kri = krs[h % 2]
                            sq = sqts[h % 4]
                            kb.actf(kfi[:, :], acc[:, :], AF.Copy, (acc.b,), (kfi.b,))
                            kb.tt(pool, sq[:, :], kfi[:, :], kfi[:, :], ALU.mult, (kfi.b,), (sq.b,))
                            pb2 = psnext()
                            kb.mm(pb2[:, :], ones_b, sq[:, :], True, True, (sq.b, mb.b), (pb2.b,))
                            kb.actf(kri[:, :], pb2[:, :], AF.Ln, (pb2.b,), (kri.b,), bias=EPS, scale=1.0 / 128)
                            kb.actf(kri[:, :], kri[:, :], AF.Exp, (kri.b,), (kri.b,), scale=-0.5)
                            kb.stt(dve, st[:, :], kfi[:, :], vcol(V_GKF), kri[:, :], ALU.mult, ALU.mult,
                                   (kfi.b, kri.b, vec.b), (st.b,))
                        else:
                            kb.cp(dve, st[:, :], acc[:, :], (acc.b,), (st.b,))
                        kb.dma(pool, kts_d[pas * 6 + h, :, tt_ * 512:(tt_ + 1) * 512], st[:, :], dss, (st.b,), ())
                    for bi in range(4):
                        blk = tt_ * 4 + bi
                        a0 = psnext()
                        a1 = psnext()
                        n1 = nv - 512
                        for k in range(KC):
                            kb.mm(a0[:, :], ht[:, k, bi * 128:(bi + 1) * 128], wV[:, k, 0:512], k == 0, k == KC - 1,
                                  (wV.b, ht.b), (a0.b,))
                        for k in range(KC):
                            kb.mm(a1[:, 0:n1], ht[:, k, bi * 128:(bi + 1) * 128], wV[:, k, 512:nv], k == 0, k == KC - 1,
                                  (wV.b, ht.b), (a1.b,))
                        vt = vst[vcnt[0] % 2]
                        dsv = ds_vs[vcnt[0] % 2]
                        vcnt[0] += 1
                        kb.actf(vt[:, 0:512], a0[:, :], AF.Copy, (a0.b,), (vt.b,))
                        kb.cp(dve, vt[:, 512:768], a1[:, 0:256], (a1.b,), (vt.b,))
                        kb.dma(pool, vs_d[pas, blk, :, :], vt[:, :], dsv, (vt.b,), ())
                        if pas == 0:
                            kb.tt(dve, ltmp[:, 0:6], a1[:, 256:262], vec[:, V_BF:V_BF + 6], ALU.add, (a1.b, vec.b), (ltmp.b,))
                            kb.actf(ltmp[:, 0:6], ltmp[:, 0:6], AF.Exp, (ltmp.b,), (ltmp.b,), scale=-1.0)
                            kb.actf(lfp[:, blk, :], ltmp[:, 0:6], AF.Ln, (ltmp.b,), (lfp.b,), bias=1.0)
                if pas == 0:
                    pc = psnext()
                    pt = psnext()
                    lf2 = lfp[:, :, :].rearrange("p a b -> p (a b)")
                    kb.mm(pc[:, 0:192], tri, lf2, True, True, (lfp.b, cf.b), (pc.b,))
                    kb.mm(pt[:, 0:192], ones_f, lf2, True, True, (lfp.b, cf.b), (pt.b,))
                    tot = T(pes, nc, "tot", [128, NB, 6], F32)
                    pre = T(pes, nc, "pre", [128, NB, 6], F32)
                    kb.cp(dve, tot[:, :, :].rearrange("p a b -> p (a b)"), pt[:, 0:192], (pt.b,), (tot.b,))
                    kb.mset(dve, pre[:, 0, :], 0.0, (pre.b,))
                    for j in range(1, NB):
                        kb.tt(dve, pre[:, j, :], pre[:, j - 1, :], tot[:, j - 1, :], ALU.add, (pre.b, tot.b), (pre.b,))
                    kb.tt(dve, ctok[:, :, :].rearrange("p a b -> p (a b)"), pc[:, 0:192],
                          pre[:, :, :].rearrange("p a b -> p (a b)"), ALU.add, (pc.b, pre.b), (ctok.b,))
                    c4 = ctok[:, :, :].rearrange("p (m r) h -> p m r h", r=4)
                    for rr in range(4):
                        if rr == 0:
                            kb.ts(dve, cown[:, :, :], c4[:, :, 0, :], vcol(V_SEL), None, ALU.mult, None, (ctok.b, vec.b), (cown.b,))
                        else:
                            kb.stt(dve, cown[:, :, :], c4[:, :, rr, :], vcol(V_SEL + rr), cown[:, :, :], ALU.mult, ALU.add,
                                   (ctok.b, vec.b, cown.b), (cown.b,))
                    kb.mset(dve, chal[:, :, :], 0.0, (chal.b,))
                    c4h = ctok[:, 3:31, :].rearrange("p (m r) h -> p m r h", r=4)
                    for rr in range(4):
                        kb.stt(dve, chal[:, 1:8, :], c4h[:, :, rr, :], vcol(V_SEL + rr), chal[:, 1:8, :], ALU.mult, ALU.add,
                               (ctok.b, vec.b, chal.b), (chal.b,))
                        if rr >= 1:
                            kb.stt(dve, chal[:, 0, :], ctok[:, rr - 1, :], vcol(V_SEL + rr), chal[:, 0, :], ALU.mult, ALU.add,
                                   (ctok.b, vec.b, chal.b), (chal.b,))
            kb.barrier()

        with ExitStack() as aes:
            hown = T(aes, nc, "hown", [128, KC, NQ], BF16)
            oT = T(aes, nc, "oT", [128, 16, NQ], BF16)
            with ExitStack() as qes:
                qf = T(qes, nc, "qf", [128, 6, NQ], BF16)
                qs = T(qes, nc, "qs", [128, 6, NQ], BF16)
                qm = T(qes, nc, "qm", [128, 4, NQ], BF16)
                mkT = T(qes, nc, "mkT", [128, 4, 256], BF16)
                mvt = T(qes, nc, "mvt", [128, 2, 512], BF16)
                with ExitStack() as pes:
                    xo = T(pes, nc, "xo", [128, KC, 512], F32)
                    wq = T(pes, nc, "wq", [128, KC, 768], BF16)
                    sqts = [T(pes, nc, "sq%d" % i, [128, 512], BF16) for i in range(4)]
                    rstd = T(pes, nc, "rstd", [128, 512], F32)
                    kf = [T(pes, nc, "kf%d" % i, [128, 512], F32) for i in range(2)]
                    krs = [T(pes, nc, "krs%d" % i, [128, 512], F32) for i in range(2)]
                    ds_xo, ds_wq = kb.dsem(), kb.dsem()
                    for (c0, c1) in COLT:
                        n = c1 - c0
                        for q4 in range(4):
                            kb.dma(sp, xo[:, q4 * 4:(q4 + 1) * 4, 0:n], xo_v[:, q4 * 4:(q4 + 1) * 4, c0:c1], ds_xo, (), (xo.b,))
                        pb = psnext()
                        rms_rstd(lambda k: (xo[:, k, 0:n], (xo.b,)), KC, n, pb, rstd, sqts, D)
                        for k in range(KC):
                            e = dve
                            kb.stt(e, hown[:, k, c0:c1], xo[:, k, 0:n], vcol(V_GMIX + k), rstd[:, 0:n], ALU.mult, ALU.mult,
                                   (xo.b, vec.b, rstd.b), (hown.b,))

                    def qproj(col0, nh, dst, mode, wsrc_v, hsrc, colt, gcol):
                        for q4 in range(4):
                            kb.dma(pool, wq[:, q4 * 4:(q4 + 1) * 4, 0:nh * 128],
                                   wsrc_v[:, q4 * 4:(q4 + 1) * 4, col0:col0 + nh * 128], ds_wq, (), (wq.b,))
                        for h in range(nh):
                            for (c0, c1) in colt:
                                n = c1 - c0
                                acc = psnext()
                                for k in range(KC):
                                    kb.mm(acc[:, 0:n], wq[:, k, h * 128:(h + 1) * 128], hsrc[:, k, c0:c1], k == 0, k == KC - 1,
                                          (wq.b, hsrc.b), (acc.b,))
                                if mode == "norm":
                                    kfi = kf[h % 2]
                                    kri = krs[h % 2]
                                    sq = sqts[h % 4]
                                    kb.actf(kfi[:, 0:n], acc[:, 0:n], AF.Copy, (acc.b,), (kfi.b,))
                                    kb.tt(pool, sq[:, 0:n], kfi[:, 0:n], kfi[:, 0:n], ALU.mult, (kfi.b,), (sq.b,))
                                    pb2 = psnext()
                                    kb.mm(pb2[:, 0:n], ones_b, sq[:, 0:n], True, True, (sq.b, mb.b), (pb2.b,))
                                    kb.actf(kri[:, 0:n], pb2[:, 0:n], AF.Ln, (pb2.b,), (kri.b,), bias=EPS, scale=1.0 / 128)
                                    kb.actf(kri[:, 0:n], kri[:, 0:n], AF.Exp, (kri.b,), (kri.b,), scale=-0.5)
                                    kb.stt(dve, dst[:, h, c0:c1], kfi[:, 0:n], gcol, kri[:, 0:n], ALU.mult, ALU.mult,
                                           (kfi.b, kri.b, vec.b, gsc.b), (dst.b,))
                                else:
                                    kb.actf(dst[:, h, c0:c1], acc[:, 0:n], AF.Copy, (acc.b,), (dst.b,), scale=SCALE)

                    qproj(C_FQ, 6, qf, "norm", w_in_v, hown, COLT, gsc[:, 0:1])
                    qproj(C_SQ, 6, qs, "scale", w_in_v, hown, COLT, None)
                    qproj(C_MQ, 4, qm, "norm", w_in_v, hown, COLT, gsc[:, 1:2])
                    hm = T(pes, nc, "hm", [128, KC, 256], BF16)
                    kb.dma(sp, xo[:, :, 0:256], mt_d.rearrange("(k p) c -> p k c", p=128), ds_xo, (), (xo.b,))
                    pb = psnext()
                    rms_rstd(lambda k: (xo[:, k, 0:256], (xo.b,)), KC, 256, pb, rstd, sqts, D)
                    for k in range(KC):
                        e = dve
                        kb.stt(e, hm[:, k, :], xo[:, k, 0:256], vcol(V_GMEM + k), rstd[:, 0:256], ALU.mult, ALU.mult,
                               (xo.b, vec.b, rstd.b), (hm.b,))
                    w_mkv_v = w_mkv_d.rearrange("(k p) c -> p k c", p=128)
                    qproj(0, 4, mkT, "norm", w_mkv_v, hm, [(0, 256)], vcol(V_GKM))
                    for q4 in range(4):
                        kb.dma(pool, wq[:, q4 * 4:(q4 + 1) * 4, 0:512], w_mkv_v[:, q4 * 4:(q4 + 1) * 4, 512:1024], ds_wq, (), (wq.b,))
                    for tb in range(2):
                        acc = psnext()
                        for k in range(KC):
                            kb.mm(acc[:, :], hm[:, k, tb * 128:(tb + 1) * 128], wq[:, k, 0:512], k == 0, k == KC - 1,
                                  (wq.b, hm.b), (acc.b,))
                        kb.actf(mvt[:, tb, :], acc[:, :], AF.Copy, (acc.b,), (mvt.b,))
                    kb.barrier()

                with ExitStack() as pes:
                    ktl = [T(pes, nc, "ktl%d" % i, [128, S], BF16) for i in range(2)]
                    vtl = [T(pes, nc, "vtl%d" % i, [128, NB, 128], BF16) for i in range(2)]
                    ds_kt = [kb.dsem(), kb.dsem()]
                    ds_vt = [kb.dsem(), kb.dsem()]
                    ncq = T(pes, nc, "ncq", [128, NQ], F32)
                    sf = [T(pes, nc, "sf%d" % i, [128, 512], F32) for i in range(2)]
                    pbf = [T(pes, nc, "pbf%d" % i, [128, 512], BF16) for i in range(2)]
                    ef = [T(pes, nc, "ef%d" % i, [128, 512], F32) for i in range(2)]
                    spb = [T(pes, nc, "spb%d" % i, [128, 512], BF16) for i in range(2)]
                    ccar = T(pes, nc, "ccar", [128, 512], F32)
                    rden = T(pes, nc, "rden", [128, 512], F32)
                    mF = mf[:, 0:512].rearrange("p (o q) -> p o q", o=4)
                    mFh = mf[:, 512:1024].rearrange("p (j q) -> p j q", j=NB)
                    mS = mb[:, MB_SM:MB_SM + 512].rearrange("p (o q) -> p o q", o=4)
                    mSh = mb[:, MB_SH:MB_SH + 512].rearrange("p (j q) -> p j q", j=NB)

                    def load_head(hh):
                        kt = ktl[hh % 2]
                        vt = vtl[hh % 2]
                        kb.dma(sp, kt[:, :], kts_d[hh, :, :], ds_kt[hh % 2], (), (kt.b,))
                        pas, h = hh // 6, hh % 6
                        for q8 in range(4):
                            kb.dma(sp, vt[:, q8 * 8:(q8 + 1) * 8, :],
                                   vs_d[pas, q8 * 8:(q8 + 1) * 8, :, h * 128:(h + 1) * 128].rearrange("b p d -> p b d"),
                                   ds_vt[hh % 2], (), (vt.b,))

                    groups = [(0, 512, 16, False, 0), (512, 512, 32, False, 1), (1024, 16, 32, True, 0)]

                    def tile_cols(j, g, halo):
                        if halo:
                            return 0, 16, True, j
                        if j < 16 * g:
                            return 0, 512, False, 0
                        ml = (j - 16 * g) // 4
                        return ml * 128, 512 - ml * 128, True, j % 4

                    load_head(0)
                    cnt = [0]
                    for hh in range(12):
                        if hh + 1 < 12:
                            load_head(hh + 1)
                        kt = ktl[hh % 2]
                        vt = vtl[hh % 2]
                        h = hh % 6
                        if hh < 6:
                            for m in range(8):
                                pq = psrot()
                                kb.mm(pq[:, 0:128], cown[:, m, h:h + 1].to_broadcast([128, 128]), ident, True, True,
                                      (cown.b, cf.b), (pq.b,))
                                kb.actf(ncq[:, m * 128:(m + 1) * 128], pq[:, 0:128], AF.Copy, (pq.b,), (ncq.b,), scale=-1.0)
                                kb.mm(pq[:, 128:130], chal[:, m, h:h + 1].to_broadcast([128, 128]), cf[:, 126:128], True, True,
                                      (chal.b, cf.b), (pq.b,))
                                kb.actf(ncq[:, 1024 + 2 * m:1026 + 2 * m], pq[:, 128:130], AF.Copy, (pq.b,), (ncq.b,), scale=-1.0)
                            for (gc0, gn, nj, halo, g) in groups:
                                OT = ps[0]
                                DEN = ps[1]
                                for j in range(nj):
                                    r0, n, msk, mi = tile_cols(j, g, halo)
                                    a0 = gc0 + r0
                                    ST = psrot()
                                    s_ = sf[cnt[0] % 2]
                                    p_ = pbf[cnt[0] % 2]
                                    cnt[0] += 1
                                    kb.mm(ST[:, 0:n], kt[:, j * 128:(j + 1) * 128], qf[:, h, a0:a0 + n], True, True,
                                          (kt.b, qf.b), (ST.b,))
                                    kb.tt(dve, s_[:, 0:n], ST[:, 0:n], ncq[:, a0:a0 + n], ALU.add, (ST.b, ncq.b), (s_.b,))
                                    if msk:
                                        if halo:
                                            kb.tt(dve, s_[:, 0:16], s_[:, 0:16], mFh[:, mi, :], ALU.add, (s_.b, mf.b), (s_.b,))
                                        else:
                                            kb.tt(dve, s_[:, 0:128], s_[:, 0:128], mF[:, mi, :], ALU.add, (s_.b, mf.b), (s_.b,))
                                    kb.actf(p_[:, 0:n], s_[:, 0:n], AF.Exp, (s_.b, ctok.b), (p_.b,), bias=ctok[:, j, h:h + 1])
                                    kb.mm(OT[:, r0:r0 + n], vt[:, j, :], p_[:, 0:n], j == 0, j == nj - 1, (vt.b, p_.b), (OT.b,))
                                    kb.mm(DEN[:, r0:r0 + n], ones_b, p_[:, 0:n], j == 0, j == nj - 1, (mb.b, p_.b), (DEN.b,))
                                kb.op(dve, lambda: nc.vector.reciprocal(rden[:, 0:gn], DEN[:, 0:gn]), (DEN.b,), (rden.b,))
                                kb.tt(dve, oT[:, hh, gc0:gc0 + gn], OT[:, 0:gn], rden[:, 0:gn], ALU.mult, (OT.b, rden.b), (oT.b,))
                        else:
                            for (gc0, gn, nj, halo, g) in groups:
                                OT = ps[0]
                                kb.mm(OT[:, 0:gn], zer_b, qs[:, h, gc0:gc0 + gn], True, True, (mb.b, qs.b), (OT.b,))
                                kb.mset(pool, ccar[:, 0:gn], 0.0, (ccar.b,))
                                for j in range(nj - 1, -1, -1):
                                    r0, n, msk, mi = tile_cols(j, g, halo)
                                    a0 = gc0 + r0
                                    Z = psrot()
                                    CS = psrot()
                                    e_ = ef[cnt[0] % 2]
                                    sp_ = spb[cnt[0] % 2]
                                    t1 = sf[cnt[0] % 2]
                                    a_ = pbf[cnt[0] % 2]
                                    cnt[0] += 1
                                    kb.mm(Z[:, 0:n], kt[:, j * 128:(j + 1) * 128], qs[:, h, a0:a0 + n], True, True,
                                          (kt.b, qs.b), (Z.b,))
                                    kb.actf(e_[:, 0:n], Z[:, 0:n], AF.Exp, (Z.b,), (e_.b,))
                                    kb.actf(sp_[:, 0:n], e_[:, 0:n], AF.Ln, (e_.b,), (sp_.b,), bias=1.0)
                                    if msk:
                                        if halo:
                                            kb.tt(pool, sp_[:, 0:16], sp_[:, 0:16], mSh[:, mi, :], ALU.mult, (sp_.b, mb.b), (sp_.b,))
                                        else:
                                            kb.tt(pool, sp_[:, 0:128], sp_[:, 0:128], mS[:, mi, :], ALU.mult, (sp_.b, mb.b), (sp_.b,))
                                    kb.mm(Z[:, 0:n], nti_b, sp_[:, 0:n], False, True, (mb.b, sp_.b), (Z.b,), skip=True)
                                    kb.mm(CS[:, 0:n], ones_b, sp_[:, 0:n], True, True, (mb.b, sp_.b), (CS.b,))
                                    kb.tt(dve, t1[:, 0:n], Z[:, 0:n], ccar[:, r0:r0 + n], ALU.subtract, (Z.b, ccar.b), (t1.b,))
                                    kb.actf(a_[:, 0:n], t1[:, 0:n], AF.Exp, (t1.b,), (a_.b,))
                                    if msk:
                                        if halo:
                                            kb.tt(pool, a_[:, 0:16], a_[:, 0:16], mSh[:, mi, :], ALU.mult, (a_.b, mb.b), (a_.b,))
                                        else:
                                            kb.tt(pool, a_[:, 0:128], a_[:, 0:128], mS[:, mi, :], ALU.mult, (a_.b, mb.b), (a_.b,))
                                    kb.tt(dve, ccar[:, r0:r0 + n], CS[:, 0:n], ccar[:, r0:r0 + n], ALU.add, (CS.b, ccar.b), (ccar.b,))
                                    kb.mm(OT[:, r0:r0 + n], vt[:, j, :], a_[:, 0:n], False, True, (vt.b, a_.b), (OT.b,), skip=True)
                                kb.actf(oT[:, hh, gc0:gc0 + gn], OT[:, 0:gn], AF.Copy, (OT.b,), (oT.b,))
                    for h in range(4):
                        for (c0, c1) in COLT:
                            n = c1 - c0
                            OT = ps[0]
                            DEN = ps[1]
                            for kbk in range(2):
                                ST = psrot()
                                p_ = pbf[cnt[0] % 2]
                                cnt[0] += 1
                                kb.mm(ST[:, 0:n], mkT[:, h, kbk * 128:(kbk + 1) * 128], qm[:, h, c0:c1], True, True,
                                      (mkT.b, qm.b), (ST.b,))
                                kb.actf(p_[:, 0:n], ST[:, 0:n], AF.Exp, (ST.b,), (p_.b,))
                                kb.mm(OT[:, 0:n], mvt[:, kbk, h * 128:(h + 1) * 128], p_[:, 0:n], kbk == 0, kbk == 1,
                                      (mvt.b, p_.b), (OT.b,))
                                kb.mm(DEN[:, 0:n], ones_b, p_[:, 0:n], kbk == 0, kbk == 1, (mb.b, p_.b), (DEN.b,))
                            kb.op(dve, lambda: nc.vector.reciprocal(rden[:, 0:n], DEN[:, 0:n]), (DEN.b,), (rden.b,))
                            kb.tt(dve, oT[:, 12 + h, c0:c1], OT[:, 0:n], rden[:, 0:n], ALU.mult, (OT.b, rden.b), (oT.b,))
                    kb.barrier()

            with ExitStack() as mes:
                mg = T(mes, nc, "mg", [128, KC, NQ], BF16)
                NSL = 6
                slots = [T(mes, nc, "ws%d" % i, [128, KC, 256], BF16) for i in range(NSL)]
                ds_sl = [kb.dsem() for _ in range(NSL)]
                sg_t = [T(mes, nc, "sg%d" % i, [128, 512], F32) for i in range(3)]
                tm_t = [T(mes, nc, "tm%d" % i, [128, 512], F32) for i in range(3)]
                xr_t = [T(mes, nc, "xr%d" % i, [128, 512], F32) for i in range(2)]
                ds_xr = [kb.dsem() for _ in range(2)]
                xs_t = [T(mes, nc, "xs%d" % i, [128, 512], F32) for i in range(2)]
                ds_xs = [kb.dsem() for _ in range(2)]
                sqts = [T(mes, nc, "sq%d" % i, [128, 512], BF16) for i in range(2)]
                w_br_v = w_br_d.rearrange("(k p) c -> p k c", p=128)
                w_out_v = w_out_d.rearrange("(k p) c -> p k c", p=128)
                blocks = []
                for fg in range(8):
                    for b in range(3):
                        blocks.append((w_in_v, C_G + b * D + fg * 256))
                    blocks.append((w_br_v, fg * 256))
                for fg in range(8):
                    blocks.append((w_out_v, fg * 256))
                loaded = []

                def ensure(i, depth):
                    while len(loaded) <= min(i + depth, len(blocks) - 1):
                        bi = len(loaded)
                        src, c0 = blocks[bi]
                        sl = slots[bi % NSL]
                        for q4 in range(2):
                            kb.dma(pool, sl[:, q4 * 8:(q4 + 1) * 8, :], src[:, q4 * 8:(q4 + 1) * 8, c0:c0 + 256],
                                   ds_sl[bi % NSL], (), (sl.b,))
                        loaded.append(sl)

                psrr[0] = 0
                for fg in range(8):
                    ensure(fg * 4, 5)
                    gw = [loaded[fg * 4 + b] for b in range(3)]
                    bw = loaded[fg * 4 + 3]
                    for fl in range(2):
                        fc = fg * 2 + fl
                        for (c0, c1) in COLT:
                            n = c1 - c0
                            G = []
                            for b in range(3):
                                acc = psnext()
                                for k in range(KC):
                                    kb.mm(acc[:, 0:n], gw[b][:, k, fl * 128:(fl + 1) * 128], hown[:, k, c0:c1], k == 0, k == KC - 1,
                                          (gw[b].b, hown.b), (acc.b,))
                                G.append(acc)
                            BR = []
                            for b, (h0, h1) in enumerate(((0, 6), (6, 12), (12, 16))):
                                acc = psnext()
                                for hh in range(h0, h1):
                                    kb.mm(acc[:, 0:n], bw[:, hh, fl * 128:(fl + 1) * 128], oT[:, hh, c0:c1], hh == h0, hh == h1 - 1,
                                          (bw.b, oT.b), (acc.b,))
                                BR.append(acc)
                            for b in range(3):
                                kb.actf(sg_t[b][:, 0:n], G[b][:, 0:n], AF.Sigmoid, (G[b].b, vec.b), (sg_t[b].b,),
                                        bias=vcol(V_BG + b * 16 + fc))
                                kb.tt(dve, tm_t[b][:, 0:n], BR[b][:, 0:n], sg_t[b][:, 0:n], ALU.mult, (BR[b].b, sg_t[b].b), (tm_t[b].b,))
                            kb.tt(pool, tm_t[0][:, 0:n], tm_t[0][:, 0:n], tm_t[1][:, 0:n], ALU.add, (tm_t[0].b, tm_t[1].b), (tm_t[0].b,))
                            kb.tt(pool, mg[:, fc, c0:c1], tm_t[0][:, 0:n], tm_t[2][:, 0:n], ALU.add, (tm_t[0].b, tm_t[2].b), (mg.b,))
                kb.barrier()
                ss2 = [ps[5], ps[6], ps[7]]
                psw = [ps[0], ps[1], ps[2], ps[3], ps[4]]
                cnt = [0]
                for fg in range(8):
                    ensure(32 + fg, 4)
                    ww = loaded[32 + fg]
                    for fl in range(2):
                        fc = fg * 2 + fl
                        for ci, (c0, c1) in enumerate(COLT):
                            n = c1 - c0
                            i2 = cnt[0] % 2
                            acc = psw[cnt[0] % 5]
                            cnt[0] += 1
                            xr = xr_t[i2]
                            xs = xs_t[i2]
                            sq = sqts[i2]
                            kb.dma(sp, xr[:, 0:n], xo_v[:, fc, c0:c1], ds_xr[i2], (), (xr.b,))
                            for k in range(KC):
                                kb.mm(acc[:, 0:n], ww[:, k, fl * 128:(fl + 1) * 128], mg[:, k, c0:c1], k == 0, k == KC - 1,
                                      (ww.b, mg.b), (acc.b,))
                            kb.tt(dve, xs[:, 0:n], acc[:, 0:n], xr[:, 0:n], ALU.add, (acc.b, xr.b), (xs.b,))
                            kb.dma(sp, xm_d[fc, :, c0:c1], xs[:, 0:n], ds_xs[i2], (xs.b,), ())
                            kb.tt(pool, sq[:, 0:n], xs[:, 0:n], xs[:, 0:n], ALU.mult, (xs.b,), (sq.b,))
                            kb.mm(ss2[ci][:, 0:n], ones_b, sq[:, 0:n], fc == 0, fc == KC - 1, (sq.b, mb.b), (ss2[ci].b,), inc=True)
                kb.barrier()
                for ci, (c0, c1) in enumerate(COLT):
                    n = c1 - c0
                    kb.actf(rs2[:, c0:c1], ss2[ci][:, 0:n], AF.Ln, (ss2[ci].b,), (rs2.b,), bias=EPS, scale=1.0 / D)
                    kb.actf(rs2[:, c0:c1], rs2[:, c0:c1], AF.Exp, (rs2.b,), (rs2.b,), scale=-0.5)
                kb.barrier()

        with ExitStack() as fes:
            aT = T(fes, nc, "aT", [128, NCH, 1024], BF16)
            with ExitStack() as ues:
                h2 = T(ues, nc, "h2", [128, KC, NQ], BF16)
                with ExitStack() as xes:
                    xl_t = [T(xes, nc, "xl%d" % i, [128, NQ], F32) for i in range(2)]
                    ds_xl = [kb.dsem() for _ in range(2)]
                    for k in range(KC):
                        xl = xl_t[k % 2]
                        kb.dma(sp, xl[:, :], xm_d[k, :, :], ds_xl[k % 2], (), (xl.b,))
                        kb.stt(dve, h2[:, k, :], xl[:, :], vcol(V_GFFN + k), rs2[:, :], ALU.mult, ALU.mult,
                               (xl.b, vec.b, rs2.b), (h2.b,))
                    kb.barrier()
                NSU = 4
                slots = [T(ues, nc, "us%d" % i, [128, KC, 256], BF16) for i in range(NSU)]
                ds_sl = [kb.dsem() for _ in range(NSU)]
                U_t = [T(ues, nc, "U%d" % i, [128, 8, 130], F32) for i in range(2)]
                y_t = [T(ues, nc, "y%d" % i, [128, 8, 128], F32) for i in range(2)]
                sg_t = [T(ues, nc, "sgl%d" % i, [128, 1024], BF16) for i in range(2)]
                w_up_v = w_up_d.rearrange("(k p) c -> p k c", p=128)
                blocks = []
                for cg in range(22):
                    blocks.append(cg * 256)
                    blocks.append(DFF + cg * 256)
                loaded = []

                def ensure_u(i, depth=3):
                    while len(loaded) <= min(i + depth, len(blocks) - 1):
                        bi = len(loaded)
                        sl = slots[bi % NSU]
                        c0 = blocks[bi]
                        for q4 in range(2):
                            kb.dma(pool, sl[:, q4 * 8:(q4 + 1) * 8, :], w_up_v[:, q4 * 8:(q4 + 1) * 8, c0:c0 + 256],
                                   ds_sl[bi % NSU], (), (sl.b,))
                        loaded.append(sl)

                psrr[0] = 0
                ucnt = [0]
                for cg in range(22):
                    ensure_u(cg * 2)
                    for cl in range(2):
                        c = cg * 2 + cl
                        sgl = sg_t[c % 2]
                        for gv in range(2):
                            wsl = loaded[cg * 2 + gv]
                            ch = c + gv * NCH
                            U = U_t[ucnt[0] % 2]
                            y = y_t[ucnt[0] % 2]
                            ucnt[0] += 1
                            for ci, (c0, c1) in enumerate(COLT):
                                n = c1 - c0
                                acc = psnext()
                                for k in range(KC):
                                    kb.mm(acc[:, 0:n], wsl[:, k, cl * 128:(cl + 1) * 128], h2[:, k, c0:c1], k == 0, k == KC - 1,
                                          (wsl.b, h2.b), (acc.b,))
                                if ci < 2:
                                    kb.actf(U[:, ci * 4:(ci + 1) * 4, 2:130], acc[:, :].rearrange("p (m t) -> p m t", m=4),
                                            AF.Copy, (acc.b,), (U.b,))
                                else:
                                    kb.cp(dve, U[:, :, 0:2], acc[:, 0:16].rearrange("p (m t) -> p m t", m=8), (acc.b,), (U.b,))
                            kb.ts(dve, U[:, 0, 0:2], U[:, 0, 0:2], vcol(V_HV), None, ALU.mult, None, (U.b, vec.b), (U.b,))
                            kb.ts(dve, y[:, :, :], U[:, :, 2:130], vcol(V_CW + 2 * 88 + ch), vcol(V_CB + ch), ALU.mult, ALU.add,
                                  (U.b, vec.b), (y.b,))
                            kb.stt(dve, y[:, :, :], U[:, :, 1:129], vcol(V_CW + 1 * 88 + ch), y[:, :, :], ALU.mult, ALU.add,
                                   (U.b, vec.b, y.b), (y.b,))
                            kb.stt(dve, y[:, :, :], U[:, :, 0:128], vcol(V_CW + ch), y[:, :, :], ALU.mult, ALU.add,
                                   (U.b, vec.b, y.b), (y.b,))
                            y2 = y[:, :, :].rearrange("p m t -> p (m t)")
                            if gv == 0:
                                kb.actf(sgl[:, :], y2, AF.Silu, (y.b,), (sgl.b,))
                            else:
                                kb.tt(dve, aT[:, c, :], sgl[:, :], y2, ALU.mult, (sgl.b, y.b), (aT.b,))
                kb.barrier()
            with ExitStack() as des:
                NSD = 6
                slots = [T(des, nc, "dsl%d" % i, [128, 11, 512], BF16) for i in range(NSD)]
                ds_sl = [kb.dsem() for _ in range(NSD)]
                xl_t = [T(des, nc, "xd%d" % i, [128, 512], F32) for i in range(3)]
                ds_xl = [kb.dsem() for _ in range(3)]
                ot_t = [T(des, nc, "ot%d" % i, [128, 512], F32) for i in range(3)]
                ds_ot = [kb.dsem() for _ in range(3)]
                w_dn_v = w_dn_d.rearrange("(c p) f -> p c f", p=128)
                loaded = []

                def ensure_d(i, depth=5):
                    while len(loaded) <= min(i + depth, 15):
                        bi = len(loaded)
                        fg, qt = bi // 4, bi % 4
                        sl = slots[bi % NSD]
                        kb.dma(pool, sl[:, :, :], w_dn_v[:, qt * 11:(qt + 1) * 11, fg * 512:(fg + 1) * 512],
                               ds_sl[bi % NSD], (), (sl.b,))
                        loaded.append(sl)

                psrr[0] = 0
                cnt = [0]
                for fg in range(4):
                    ensure_d(fg * 4)
                    wq4 = [loaded[fg * 4 + q] for q in range(4)]
                    for fl in range(4):
                        fc = fg * 4 + fl
                        for (c0, c1) in COLT[:2]:
                            i3 = cnt[0] % 3
                            cnt[0] += 1
                            xl = xl_t[i3]
                            ot = ot_t[i3]
                            kb.dma(sp, xl[:, :], xm_d[fc, :, c0:c1], ds_xl[i3], (), (xl.b,))
                            acc = psnext()
                            for c in range(NCH):
                                wsl = wq4[c // 11]
                                kb.mm(acc[:, :], wsl[:, c % 11, fl * 128:(fl + 1) * 128], aT[:, c, c0:c1], c == 0, c == NCH - 1,
                                      (wsl.b, aT.b), (acc.b,))
                            kb.tt(dve, ot[:, :], acc[:, :], xl[:, :], ALU.add, (acc.b, xl.b), (ot.b,))
                            kb.dma(sp, out_d[fc * 128:(fc + 1) * 128, c0:c1], ot[:, :], ds_ot[i3], (ot.b,), ())
                kb.barrier()
    return nc


def _host_inputs(inputs):
    f32 = np.float32
    x = np.asarray(inputs["x"], f32)
    mem = np.asarray(inputs["mem"], f32)
    w_in = np.ascontiguousarray(np.asarray(inputs["w_in"], f32)[0])
    w_mkv = np.ascontiguousarray(np.asarray(inputs["w_mem_kv"], f32)[0])
    w_br = np.ascontiguousarray(np.concatenate([np.asarray(inputs["w_br_fox"], f32)[0], np.asarray(inputs["w_br_sb"], f32)[0],
                                                np.asarray(inputs["w_br_mem"], f32)[0]], axis=0))
    w_out = np.ascontiguousarray(np.asarray(inputs["w_out"], f32)[0])
    w_up = np.ascontiguousarray(np.asarray(inputs["w_up"], f32)[0])
    w_dn = np.ascontiguousarray(np.asarray(inputs["w_down"], f32)[0])
    chunkT = lambda v: np.asarray(v, f32).reshape(-1, 128).T
    vec0 = np.zeros((128, NV), f32)
    vec0[:, V_GMIX:V_GMIX + 16] = chunkT(inputs["g_mix"][0])
    vec0[:, V_GFFN:V_GFFN + 16] = chunkT(inputs["g_ffn"][0])
    vec0[:, V_GMEM:V_GMEM + 16] = chunkT(inputs["g_mem"][0])
    vec0[:, V_GQF] = np.asarray(inputs["g_q_fox"], f32)[0]
    vec0[:, V_GKF] = np.asarray(inputs["g_k_fox"], f32)[0]
    vec0[:, V_GQM] = np.asarray(inputs["g_q_mem"], f32)[0]
    vec0[:, V_GKM] = np.asarray(inputs["g_k_mem"], f32)[0]
    bg = np.asarray(inputs["b_gate"], f32)[0]
    for b in range(3):
        vec0[:, V_BG + b * 16:V_BG + (b + 1) * 16] = chunkT(bg[b])
    cw = np.asarray(inputs["conv_w"], f32)[0]
    for k in range(3):
        vec0[:, V_CW + k * 88:V_CW + (k + 1) * 88] = chunkT(cw[k])
    vec0[:, V_CB:V_CB + 88] = chunkT(np.asarray(inputs["conv_b"], f32)[0])
    vec0[:, V_BF:V_BF + 6] = np.asarray(inputs["b_forget"], f32)[0][None, :]
    p = np.arange(128)
    cf = np.zeros((128, 384), f32)
    cf[:, 0:128] = np.eye(128, dtype=f32)
    cf[:, 128:256] = (p[:, None] <= p[None, :]).astype(f32)
    cf[:, 256:384] = 1.0
    in_maps = []
    for c in range(8):
        b, r = c // 4, c % 4
        vec = vec0.copy()
        vec[:, V_SEL + r] = 1.0
        vec[:, V_HV] = 0.0 if r == 0 else 1.0
        pos_main = ((4 * np.arange(8)[:, None] + r) * 128 + p[None, :]).reshape(-1)
        pos_halo = ((4 * np.arange(8)[:, None] + r) * 128 - 2 + np.arange(2)[None, :]).reshape(-1)
        xo = np.zeros((D, NQ), f32)
        xo[:, :1024] = x[b, pos_main, :].T
        valid = pos_halo >= 0
        xo[:, 1024 + np.nonzero(valid)[0]] = x[b, pos_halo[valid], :].T
        xa = np.ascontiguousarray(x[b].T)
        mt = np.ascontiguousarray(mem[b].T)
        mf = np.zeros((128, 1024), f32)
        mb = np.zeros((128, NMB), f32)
        mb[:, MB_ONES:MB_ONES + 128] = 1.0
        mb[:, MB_NTI:MB_NTI + 128] = -(p[:, None] >= p[None, :]).astype(f32)
        for o in range(4):
            if o < r:
                fm = np.zeros((128, 128), f32)
                sm = np.ones((128, 128), f32)
            elif o == r:
                fm = np.where(p[:, None] <= p[None, :], 0.0, NEG).astype(f32)
                sm = (p[:, None] < p[None, :]).astype(f32)
            else:
                fm = np.full((128, 128), NEG, f32)
                sm = np.zeros((128, 128), f32)
            mf[:, o * 128:(o + 1) * 128] = fm
            mb[:, MB_SM + o * 128:MB_SM + (o + 1) * 128] = sm
        for j in range(NB):
            kpos = j * 128 + p
            for i, qp in enumerate(pos_halo):
                if qp >= 0:
                    mf[:, 512 + j * 16 + i] = np.where(kpos <= qp, 0.0, NEG)
                    mb[:, MB_SH + j * 16 + i] = (kpos < qp).astype(f32)
                else:
                    mf[:, 512 + j * 16 + i] = 0.0 if j == 0 else NEG
                    mb[:, MB_SH + j * 16 + i] = 0.0
        in_maps.append({
            "xo": xo, "xa": xa, "mt": mt, "w_in": w_in, "w_mkv": w_mkv, "w_br": w_br, "w_out": w_out,
            "w_up": w_up, "w_dn": w_dn, "vec": vec, "cf": cf, "mf": mf, "mb": mb.astype(ml_dtypes.bfloat16),
        })
    return in_maps


_NC_CACHE = {}


def kernel(**inputs):
    in_maps = _host_inputs(inputs)
    if "nc" not in _NC_CACHE:
        _NC_CACHE["nc"] = build_program()
    nc = _NC_CACHE["nc"]
    res = run_bass_kernel_spmd(nc, in_maps, core_ids=list(range(8)))
    out = np.zeros((2, S, D), np.float32)
    p = np.arange(128)
    for c in range(8):
        b, r = c // 4, c % 4
        pos_main = ((4 * np.arange(8)[:, None] + r) * 128 + p[None, :]).reshape(-1)
        out[b, pos_main, :] = np.asarray(res.results[c]["outT"]).T
    return out
```

```python
import numpy as np
import ml_dtypes
from contextlib import ExitStack
import concourse.bass as bass
import concourse.mybir as mybir
from concourse.bass_utils import run_bass_kernel_spmd

F32 = mybir.dt.float32
BF16 = mybir.dt.bfloat16
AF = mybir.ActivationFunctionType
ALU = mybir.AluOpType

D = 2048
KC = 16
S = 4096
NB = 32
NQ = 1040
COLT = [(0, 512), (512, 1024), (1024, 1040)]
DFF = 5632
NCH = 44
EPS = 1e-6
SCALE = 128 ** -0.5
NEG = -30000.0
IN_COLS = 11270
C_FQ, C_FK, C_FV, C_FL, C_SQ, C_SK, C_SV, C_MQ, C_G = 0, 768, 1536, 2304, 2310, 3078, 3846, 4614, 5126

V_GMIX, V_GFFN, V_GMEM, V_GQF, V_GKF, V_GQM, V_GKM = 0, 16, 32, 48, 49, 50, 51
V_BG, V_CW, V_CB, V_BF, V_SEL, V_HV, NV = 52, 100, 364, 452, 458, 462, 464
MB_ONES, MB_NTI, MB_Z, MB_SM, MB_SH, NMB = 0, 128, 256, 384, 896, 1408


class Buf:
    __slots__ = ("w", "r")

    def __init__(self):
        self.w = None
        self.r = {}


class Eng:
    def __init__(self, h, sem):
        self.h = h
        self.sem = sem
        self.cnt = 0
        self.waited = {}

    def wait(self, tok):
        if tok is None:
            return
        sem, val = tok
        k = id(sem)
        if self.waited.get(k, 0) >= val:
            return
        self.h.wait_ge(sem, val)
        self.waited[k] = val


class DSem:
    def __init__(self, sem):
        self.sem = sem
        self.val = 0


class KB:
    def __init__(self, nc, es):
        self.nc = nc
        self.es = es
        mk = lambda n: es.enter_context(nc.semaphore(n))
        self.pe = Eng(nc.tensor, mk("s_pe"))
        self.act = Eng(nc.scalar, mk("s_act"))
        self.dve = Eng(nc.vector, mk("s_dve"))
        self.pool = Eng(nc.gpsimd, mk("s_pool"))
        self.sp = Eng(nc.sync, mk("s_sp"))
        self.engs = [self.pe, self.act, self.dve, self.pool, self.sp]
        self.dsems = []
        self.nsem = 0

    def dsem(self):
        self.nsem += 1
        d = DSem(self.es.enter_context(self.nc.semaphore("ds%d" % self.nsem)))
        self.dsems.append(d)
        return d

    def _pre(self, e, reads, writes):
        for b in reads:
            if b.w is not None and not (e is self.pe and b.w[0] is e.sem):
                e.wait(b.w)
        for b in writes:
            for tok in b.r.values():
                if tok[0] is not e.sem:
                    e.wait(tok)
            if b.w is not None and b.w[0] is not e.sem:
                e.wait(b.w)

    def _post(self, tok, reads, writes):
        for b in reads:
            k = id(tok[0])
            if k not in b.r or b.r[k][1] < tok[1]:
                b.r[k] = tok
        for b in writes:
            b.w = tok
            b.r = {}

    def op(self, e, emit, reads=(), writes=(), inc=True):
        self._pre(e, reads, writes)
        ins = emit()
        if inc:
            e.cnt += 1
            ins.then_inc(e.sem, 1)
            tok = (e.sem, e.cnt)
        else:
            tok = (e.sem, e.cnt + 1)
        self._post(tok, reads, writes)

    def dma(self, q, out, in_, ds, reads=(), writes=()):
        for b in writes:
            if b.w is not None and b.w[0] is ds.sem:
                b.w = None
        self._pre(q, reads, writes)
        ds.val += 16
        q.h.dma_start(out=out, in_=in_).then_inc(ds.sem, 16)
        self._post((ds.sem, ds.val), reads, writes)

    def barrier(self):
        for e in self.engs:
            for o in self.engs:
                if o is not e and o.cnt > 0:
                    e.wait((o.sem, o.cnt))
            for d in self.dsems:
                if d.val > 0:
                    e.wait((d.sem, d.val))

    def mm(self, out, lhsT, rhs, start, stop, reads, writes, inc=None, skip=False):
        if inc is None:
            inc = stop
        if skip:
            self.op(self.pe, lambda: self.nc.tensor.matmul(out, lhsT, rhs, start=start, stop=stop, skip_group_check=True),
                    reads, writes, inc)
        else:
            self.op(self.pe, lambda: self.nc.tensor.matmul(out, lhsT, rhs, start=start, stop=stop), reads, writes, inc)

    def actf(self, out, in_, func, reads, writes, bias=0.0, scale=1.0):
        self.op(self.act, lambda: self.nc.scalar.activation(out=out, in_=in_, func=func, bias=bias, scale=scale), reads, writes)

    def tt(self, e, out, in0, in1, op, reads, writes):
        self.op(e, lambda: e.h.tensor_tensor(out=out, in0=in0, in1=in1, op=op), reads, writes)

    def ts(self, e, out, in0, s1, s2, op0, op1, reads, writes):
        if s2 is None:
            self.op(e, lambda: e.h.tensor_scalar(out=out, in0=in0, scalar1=s1, scalar2=None, op0=op0), reads, writes)
        else:
            self.op(e, lambda: e.h.tensor_scalar(out=out, in0=in0, scalar1=s1, scalar2=s2, op0=op0, op1=op1), reads, writes)

    def stt(self, e, out, in0, scalar, in1, op0, op1, reads, writes):
        self.op(e, lambda: e.h.scalar_tensor_tensor(out=out, in0=in0, scalar=scalar, in1=in1, op0=op0, op1=op1), reads, writes)

    def cp(self, e, out, in_, reads, writes):
        self.op(e, lambda: e.h.tensor_copy(out=out, in_=in_), reads, writes)

    def mset(self, e, ap, val, writes):
        self.op(e, lambda: e.h.memset(ap, val), (), writes)


class T:
    _n = [0]

    def __init__(self, es, nc, name, shape, dt, psum=False):
        T._n[0] += 1
        name = "t%d_%s" % (T._n[0], name)
        if psum:
            self.t = es.enter_context(nc.psum_tensor(name, shape, dt))
        else:
            self.t = es.enter_context(nc.sbuf_tensor(name, shape, dt))
        self.b = Buf()

    def __getitem__(self, idx):
        return self.t[idx]


def build_program():
    nc = bass.Bass("TRN2", target_bir_lowering=False)
    dr = lambda n, s, dt, kind="ExternalInput": nc.dram_tensor(n, list(s), dt, kind=kind).ap()
    xo_d = dr("xo", (D, NQ), F32)
    xa_d = dr("xa", (D, S), F32)
    mt_d = dr("mt", (D, 256), F32)
    w_in_d = dr("w_in", (D, IN_COLS), F32)
    w_mkv_d = dr("w_mkv", (D, 1024), F32)
    w_br_d = dr("w_br", (D, D), F32)
    w_out_d = dr("w_out", (D, D), F32)
    w_up_d = dr("w_up", (D, 2 * DFF), F32)
    w_dn_d = dr("w_dn", (DFF, D), F32)
    vec_d = dr("vec", (128, NV), F32)
    cf_d = dr("cf", (128, 384), F32)
    mf_d = dr("mf", (128, 1024), F32)
    mb_d = dr("mb", (128, NMB), BF16)
    out_d = dr("outT", (D, 1024), F32, kind="ExternalOutput")
    kts_d = dr("kts", (12, 128, S), BF16, kind="Internal")
    vs_d = dr("vs", (2, NB, 128, 768), BF16, kind="Internal")
    xm_d = dr("xmid", (KC, 128, NQ), F32, kind="Internal")

    w_in_v = w_in_d.rearrange("(k p) c -> p k c", p=128)
    xa_v = xa_d.rearrange("(k p) c -> p k c", p=128)
    xo_v = xo_d.rearrange("(k p) c -> p k c", p=128)

    es = ExitStack()
    with es:
        kb = KB(nc, es)
        pe, act, dve, pool, sp = kb.pe, kb.act, kb.dve, kb.pool, kb.sp
        vec = T(es, nc, "vec", [128, NV], F32)
        cf = T(es, nc, "cf", [128, 384], F32)
        mf = T(es, nc, "mf", [128, 1024], F32)
        mb = T(es, nc, "mb", [128, NMB], BF16)
        gsc = T(es, nc, "gsc", [128, 2], F32)
        ctok = T(es, nc, "ctok", [128, NB, 6], F32)
        cown = T(es, nc, "cown", [128, 8, 6], F32)
        chal = T(es, nc, "chal", [128, 8, 6], F32)
        ps = [T(es, nc, "ps%d" % i, [128, 512], F32, psum=True) for i in range(8)]
        for tl, dd in ((vec, vec_d), (cf, cf_d), (mf, mf_d), (mb, mb_d)):
            kb.dma(sp, tl[:, :], dd, kb.dsem(), (), (tl.b,))
        ident = cf[:, 0:128]
        tri = cf[:, 128:256]
        ones_f = cf[:, 256:384]
        ones_b = mb[:, MB_ONES:MB_ONES + 128]
        nti_b = mb[:, MB_NTI:MB_NTI + 128]
        zer_b = mb[:, MB_Z:MB_Z + 128]
        vcol = lambda i: vec[:, i:i + 1]
        kb.ts(dve, gsc[:, 0:1], vcol(V_GQF), SCALE, None, ALU.mult, None, (vec.b,), (gsc.b,))
        kb.ts(dve, gsc[:, 1:2], vcol(V_GQM), SCALE, None, ALU.mult, None, (vec.b,), (gsc.b,))

        rs2 = T(es, nc, "rs2", [128, NQ], F32)
        psrr = [0]
        psrr2 = [0]

        def psnext():
            t = ps[psrr[0] % 8]
            psrr[0] += 1
            return t

        def psrot():
            t = ps[2 + psrr2[0] % 6]
            psrr2[0] += 1
            return t

        def rms_rstd(sq_src_fn, nchunks, ncols, pbank, rstd_t, sqts, n_feat, alt=True):
            for k in range(nchunks):
                src, srcb = sq_src_fn(k)
                sq = sqts[k % len(sqts)]
                e = dve if (k % 2 == 0 or not alt) else pool
                kb.tt(e, sq[:, :ncols], src, src, ALU.mult, srcb, (sq.b,))
                kb.mm(pbank[:, :ncols], ones_b, sq[:, :ncols], k == 0, k == nchunks - 1, (sq.b, mb.b), (pbank.b,), inc=True)
            kb.actf(rstd_t[:, :ncols], pbank[:, :ncols], AF.Ln, (pbank.b,), (rstd_t.b,), bias=EPS, scale=1.0 / n_feat)
            kb.actf(rstd_t[:, :ncols], rstd_t[:, :ncols], AF.Exp, (rstd_t.b,), (rstd_t.b,), scale=-0.5)

        with ExitStack() as pes:
            wK = T(pes, nc, "wK", [128, KC, 768], BF16)
            wV = T(pes, nc, "wV", [128, KC, 774], BF16)
            xts = [T(pes, nc, "xt%d" % i, [128, KC, 512], F32) for i in range(2)]
            hts = [T(pes, nc, "ht%d" % i, [128, KC, 512], BF16) for i in range(2)]
            sqts = [T(pes, nc, "sq%d" % i, [128, 512], BF16) for i in range(4)]
            rstd = T(pes, nc, "rstd", [128, 512], F32)
            kf = [T(pes, nc, "kf%d" % i, [128, 512], F32) for i in range(2)]
            krs = [T(pes, nc, "krs%d" % i, [128, 512], F32) for i in range(2)]
            kst = [T(pes, nc, "kst%d" % i, [128, 512], BF16) for i in range(3)]
            vst = [T(pes, nc, "vst%d" % i, [128, 768], BF16) for i in range(2)]
            lfp = T(pes, nc, "lfp", [128, NB, 6], F32)
            ltmp = T(pes, nc, "ltmp", [128, 8], F32)
            ds_wk, ds_wv = kb.dsem(), kb.dsem()
            ds_x = [kb.dsem(), kb.dsem()]
            ds_ks = [kb.dsem() for _ in range(3)]
            ds_vs = [kb.dsem() for _ in range(2)]
            kcnt = [0]
            vcnt = [0]
            pend = [None]
            for pas in range(2):
                ck = C_FK if pas == 0 else C_SK
                cv = C_FV if pas == 0 else C_SV
                nv = 774 if pas == 0 else 768
                for q4 in range(4):
                    kb.dma(pool, wK[:, q4 * 4:(q4 + 1) * 4, :], w_in_v[:, q4 * 4:(q4 + 1) * 4, ck:ck + 768], ds_wk, (), (wK.b,))
                for q4 in range(4):
                    kb.dma(pool, wV[:, q4 * 4:(q4 + 1) * 4, 0:nv], w_in_v[:, q4 * 4:(q4 + 1) * 4, cv:cv + nv], ds_wv, (), (wV.b,))

                def load_x(tt_):
                    xt = xts[tt_ % 2]
                    for q4 in range(4):
                        kb.dma(sp, xt[:, q4 * 4:(q4 + 1) * 4, :], xa_v[:, q4 * 4:(q4 + 1) * 4, tt_ * 512:(tt_ + 1) * 512],
                               ds_x[tt_ % 2], (), (xt.b,))

                load_x(0)
                for tt_ in range(8):
                    if tt_ + 1 < 8:
                        load_x(tt_ + 1)
                    xt = xts[tt_ % 2]
                    ht = hts[tt_ % 2]
                    pb = psnext()
                    rms_rstd(lambda k: (xt[:, k, :], (xt.b,)), KC, 512, pb, rstd, sqts, D)
                    for k in range(KC):
                        e = dve
                        kb.stt(e, ht[:, k, :], xt[:, k, :], vcol(V_GMIX + k), rstd[:, :], ALU.mult, ALU.mult,
                               (xt.b, vec.b, rstd.b), (ht.b,))
                    for h in range(6):
                        acc = psnext()
                        for k in range(KC):
                            kb.mm(acc[:, :], wK[:, k, h * 128:(h + 1) * 128], ht[:, k, :], k == 0, k == KC - 1,
                                  (wK.b, ht.b), (acc.b,))
                        st = kst[kcnt[0] % 3]
                        dss = ds_ks[kcnt[0] % 3]
                        kcnt[0] += 1
                        if pas == 0:
                            kfi = kf[h % 2]
                            kri = krs[h % 2]
                            sq = sqts[h % 4]
                            kb.actf(kfi[:, :], acc[:, :], AF.Copy, (acc.b,), (kfi.b,))
                            kb.tt(pool, sq[:, :], kfi[:, :], kfi[:, :], ALU.mult, (kfi.b,), (sq.b,))

                            def tail(kfi=kfi, kri=kri, sq=sq, st=st, dss=dss, h=h, tt_=tt_):
                                pb2 = psnext()
                                kb.mm(pb2[:, :], ones_b, sq[:, :], True, True, (sq.b, mb.b), (pb2.b,))
                                kb.actf(kri[:, :], pb2[:, :], AF.Ln, (pb2.b,), (kri.b,), bias=EPS, scale=1.0 / 128)
                                kb.actf(kri[:, :], kri[:, :], AF.Exp, (kri.b,), (kri.b,), scale=-0.5)
                                kb.stt(dve, st[:, :], kfi[:, :], vcol(V_GKF), kri[:, :], ALU.mult, ALU.mult,
                                       (kfi.b, kri.b, vec.b), (st.b,))
                                kb.dma(pool, kts_d[h, :, tt_ * 512:(tt_ + 1) * 512], st[:, :], dss, (st.b,), ())

                            if pend[0] is not None:
                                pend[0]()
                            pend[0] = tail
                        else:
                            kb.cp(dve, st[:, :], acc[:, :], (acc.b,), (st.b,))
                            kb.dma(pool, kts_d[pas * 6 + h, :, tt_ * 512:(tt_ + 1) * 512], st[:, :], dss, (st.b,), ())
                    for bi in range(4):
                        if bi == 1 and pend[0] is not None:
                            pend[0]()
                            pend[0] = None
                        blk = tt_ * 4 + bi
                        a0 = psnext()
                        a1 = psnext()
                        n1 = nv - 512
                        for k in range(KC):
                            kb.mm(a0[:, :], ht[:, k, bi * 128:(bi + 1) * 128], wV[:, k, 0:512], k == 0, k == KC - 1,
                                  (wV.b, ht.b), (a0.b,))
                        for k in range(KC):
                            kb.mm(a1[:, 0:n1], ht[:, k, bi * 128:(bi + 1) * 128], wV[:, k, 512:nv], k == 0, k == KC - 1,
                                  (wV.b, ht.b), (a1.b,))
                        vt = vst[vcnt[0] % 2]
                        dsv = ds_vs[vcnt[0] % 2]
                        vcnt[0] += 1
                        kb.actf(vt[:, 0:512], a0[:, :], AF.Copy, (a0.b,), (vt.b,))
                        kb.cp(dve, vt[:, 512:768], a1[:, 0:256], (a1.b,), (vt.b,))
                        kb.dma(pool, vs_d[pas, blk, :, :], vt[:, :], dsv, (vt.b,), ())
                        if pas == 0:
                            kb.tt(dve, ltmp[:, 0:6], a1[:, 256:262], vec[:, V_BF:V_BF + 6], ALU.add, (a1.b, vec.b), (ltmp.b,))
                            kb.actf(ltmp[:, 0:6], ltmp[:, 0:6], AF.Exp, (ltmp.b,), (ltmp.b,), scale=-1.0)
                            kb.actf(lfp[:, blk, :], ltmp[:, 0:6], AF.Ln, (ltmp.b,), (lfp.b,), bias=1.0)
                if pas == 0:
                    pc = psnext()
                    pt = psnext()
                    lf2 = lfp[:, :, :].rearrange("p a b -> p (a b)")
                    kb.mm(pc[:, 0:192], tri, lf2, True, True, (lfp.b, cf.b), (pc.b,))
                    kb.mm(pt[:, 0:192], ones_f, lf2, True, True, (lfp.b, cf.b), (pt.b,))
                    tot = T(pes, nc, "tot", [128, NB, 6], F32)
                    pre = T(pes, nc, "pre", [128, NB, 6], F32)
                    kb.cp(dve, tot[:, :, :].rearrange("p a b -> p (a b)"), pt[:, 0:192], (pt.b,), (tot.b,))
                    kb.mset(dve, pre[:, 0, :], 0.0, (pre.b,))
                    for j in range(1, NB):
                        kb.tt(dve, pre[:, j, :], pre[:, j - 1, :], tot[:, j - 1, :], ALU.add, (pre.b, tot.b), (pre.b,))
                    kb.tt(dve, ctok[:, :, :].rearrange("p a b -> p (a b)"), pc[:, 0:192],
                          pre[:, :, :].rearrange("p a b -> p (a b)"), ALU.add, (pc.b, pre.b), (ctok.b,))
                    c4 = ctok[:, :, :].rearrange("p (m r) h -> p m r h", r=4)
                    for rr in range(4):
                        if rr == 0:
                            kb.ts(dve, cown[:, :, :], c4[:, :, 0, :], vcol(V_SEL), None, ALU.mult, None, (ctok.b, vec.b), (cown.b,))
                        else:
                            kb.stt(dve, cown[:, :, :], c4[:, :, rr, :], vcol(V_SEL + rr), cown[:, :, :], ALU.mult, ALU.add,
                                   (ctok.b, vec.b, cown.b), (cown.b,))
                    kb.mset(dve, chal[:, :, :], 0.0, (chal.b,))
                    c4h = ctok[:, 3:31, :].rearrange("p (m r) h -> p m r h", r=4)
                    for rr in range(4):
                        kb.stt(dve, chal[:, 1:8, :], c4h[:, :, rr, :], vcol(V_SEL + rr), chal[:, 1:8, :], ALU.mult, ALU.add,
                               (ctok.b, vec.b, chal.b), (chal.b,))
                        if rr >= 1:
                            kb.stt(dve, chal[:, 0, :], ctok[:, rr - 1, :], vcol(V_SEL + rr), chal[:, 0, :], ALU.mult, ALU.add,
                                   (ctok.b, vec.b, chal.b), (chal.b,))
            kb.barrier()

        with ExitStack() as aes:
            hown = T(aes, nc, "hown", [128, KC, NQ], BF16)
            oT = T(aes, nc, "oT", [128, 16, NQ], BF16)
            with ExitStack() as qes:
                qf = T(qes, nc, "qf", [128, 6, NQ], BF16)
                qs = T(qes, nc, "qs", [128, 6, NQ], BF16)
                qm = T(qes, nc, "qm", [128, 4, NQ], BF16)
                mkT = T(qes, nc, "mkT", [128, 4, 256], BF16)
                mvt = T(qes, nc, "mvt", [128, 2, 512], BF16)
                with ExitStack() as pes:
                    xo = T(pes, nc, "xo", [128, KC, 512], F32)
                    wq = T(pes, nc, "wq", [128, KC, 768], BF16)
                    sqts = [T(pes, nc, "sq%d" % i, [128, 512], BF16) for i in range(4)]
                    rstd = T(pes, nc, "rstd", [128, 512], F32)
                    kf = [T(pes, nc, "kf%d" % i, [128, 512], F32) for i in range(2)]
                    krs = [T(pes, nc, "krs%d" % i, [128, 512], F32) for i in range(2)]
                    ds_xo, ds_wq = kb.dsem(), kb.dsem()
                    for (c0, c1) in COLT:
                        n = c1 - c0
                        for q4 in range(4):
                            kb.dma(sp, xo[:, q4 * 4:(q4 + 1) * 4, 0:n], xo_v[:, q4 * 4:(q4 + 1) * 4, c0:c1], ds_xo, (), (xo.b,))
                        pb = psnext()
                        rms_rstd(lambda k: (xo[:, k, 0:n], (xo.b,)), KC, n, pb, rstd, sqts, D)
                        for k in range(KC):
                            e = dve
                            kb.stt(e, hown[:, k, c0:c1], xo[:, k, 0:n], vcol(V_GMIX + k), rstd[:, 0:n], ALU.mult, ALU.mult,
                                   (xo.b, vec.b, rstd.b), (hown.b,))

                    def qproj(col0, nh, dst, mode, wsrc_v, hsrc, colt, gcol):
                        for q4 in range(4):
                            kb.dma(pool, wq[:, q4 * 4:(q4 + 1) * 4, 0:nh * 128],
                                   wsrc_v[:, q4 * 4:(q4 + 1) * 4, col0:col0 + nh * 128], ds_wq, (), (wq.b,))
                        for h in range(nh):
                            for (c0, c1) in colt:
                                n = c1 - c0
                                acc = psnext()
                                for k in range(KC):
                                    kb.mm(acc[:, 0:n], wq[:, k, h * 128:(h + 1) * 128], hsrc[:, k, c0:c1], k == 0, k == KC - 1,
                                          (wq.b, hsrc.b), (acc.b,))
                                if mode == "norm":
                                    kfi = kf[h % 2]
                                    kri = krs[h % 2]
                                    sq = sqts[h % 4]
                                    kb.actf(kfi[:, 0:n], acc[:, 0:n], AF.Copy, (acc.b,), (kfi.b,))
                                    kb.tt(pool, sq[:, 0:n], kfi[:, 0:n], kfi[:, 0:n], ALU.mult, (kfi.b,), (sq.b,))
                                    pb2 = psnext()
                                    kb.mm(pb2[:, 0:n], ones_b, sq[:, 0:n], True, True, (sq.b, mb.b), (pb2.b,))
                                    kb.actf(kri[:, 0:n], pb2[:, 0:n], AF.Ln, (pb2.b,), (kri.b,), bias=EPS, scale=1.0 / 128)
                                    kb.actf(kri[:, 0:n], kri[:, 0:n], AF.Exp, (kri.b,), (kri.b,), scale=-0.5)
                                    kb.stt(dve, dst[:, h, c0:c1], kfi[:, 0:n], gcol, kri[:, 0:n], ALU.mult, ALU.mult,
                                           (kfi.b, kri.b, vec.b, gsc.b), (dst.b,))
                                else:
                                    kb.actf(dst[:, h, c0:c1], acc[:, 0:n], AF.Copy, (acc.b,), (dst.b,), scale=SCALE)

                    qproj(C_FQ, 6, qf, "norm", w_in_v, hown, COLT, gsc[:, 0:1])
                    qproj(C_SQ, 6, qs, "scale", w_in_v, hown, COLT, None)
                    qproj(C_MQ, 4, qm, "norm", w_in_v, hown, COLT, gsc[:, 1:2])
                    hm = T(pes, nc, "hm", [128, KC, 256], BF16)
                    kb.dma(sp, xo[:, :, 0:256], mt_d.rearrange("(k p) c -> p k c", p=128), ds_xo, (), (xo.b,))
                    pb = psnext()
                    rms_rstd(lambda k: (xo[:, k, 0:256], (xo.b,)), KC, 256, pb, rstd, sqts, D)
                    for k in range(KC):
                        e = dve
                        kb.stt(e, hm[:, k, :], xo[:, k, 0:256], vcol(V_GMEM + k), rstd[:, 0:256], ALU.mult, ALU.mult,
                               (xo.b, vec.b, rstd.b), (hm.b,))
                    w_mkv_v = w_mkv_d.rearrange("(k p) c -> p k c", p=128)
                    qproj(0, 4, mkT, "norm", w_mkv_v, hm, [(0, 256)], vcol(V_GKM))
                    for q4 in range(4):
                        kb.dma(pool, wq[:, q4 * 4:(q4 + 1) * 4, 0:512], w_mkv_v[:, q4 * 4:(q4 + 1) * 4, 512:1024], ds_wq, (), (wq.b,))
                    for tb in range(2):
                        acc = psnext()
                        for k in range(KC):
                            kb.mm(acc[:, :], hm[:, k, tb * 128:(tb + 1) * 128], wq[:, k, 0:512], k == 0, k == KC - 1,
                                  (wq.b, hm.b), (acc.b,))
                        kb.actf(mvt[:, tb, :], acc[:, :], AF.Copy, (acc.b,), (mvt.b,))
                    kb.barrier()

                with ExitStack() as pes:
                    ktl = [T(pes, nc, "ktl%d" % i, [128, S], BF16) for i in range(2)]
                    vtl = [T(pes, nc, "vtl%d" % i, [128, NB, 128], BF16) for i in range(2)]
                    ds_kt = [kb.dsem(), kb.dsem()]
                    ds_vt = [kb.dsem(), kb.dsem()]
                    ncqs = [T(pes, nc, "ncq%d" % i, [128, NQ], F32) for i in range(2)]
                    sf = [T(pes, nc, "sf%d" % i, [128, 512], F32) for i in range(3)]
                    pbf = [T(pes, nc, "pbf%d" % i, [128, 512], BF16) for i in range(4)]
                    ef = [T(pes, nc, "ef%d" % i, [128, 512], F32) for i in range(2)]
                    spb = [T(pes, nc, "spb%d" % i, [128, 512], BF16) for i in range(4)]
                    ccar = T(pes, nc, "ccar", [128, 512], F32)
                    rden = T(pes, nc, "rden", [128, 512], F32)
                    mF = mf[:, 0:512].rearrange("p (o q) -> p o q", o=4)
                    mFh = mf[:, 512:1024].rearrange("p (j q) -> p j q", j=NB)
                    mS = mb[:, MB_SM:MB_SM + 512].rearrange("p (o q) -> p o q", o=4)
                    mSh = mb[:, MB_SH:MB_SH + 512].rearrange("p (j q) -> p j q", j=NB)

                    def load_head(hh):
                        kt = ktl[hh % 2]
                        vt = vtl[hh % 2]
                        kb.dma(sp, kt[:, :], kts_d[hh, :, :], ds_kt[hh % 2], (), (kt.b,))
                        pas, h = hh // 6, hh % 6
                        for q8 in range(4):
                            kb.dma(sp, vt[:, q8 * 8:(q8 + 1) * 8, :],
                                   vs_d[pas, q8 * 8:(q8 + 1) * 8, :, h * 128:(h + 1) * 128].rearrange("b p d -> p b d"),
                                   ds_vt[hh % 2], (), (vt.b,))

                    groups = [(0, 512, 16, False, 0), (512, 512, 32, False, 1), (1024, 16, 32, True, 0)]

                    def tile_cols(j, g, halo):
                        if halo:
                            return 0, 16, True, j
                        if j < 16 * g:
                            return 0, 512, False, 0
                        ml = (j - 16 * g) // 4
                        return ml * 128, 512 - ml * 128, True, j % 4

                    tiles = []
                    gi = 0
                    for hh in range(12):
                        for (gc0, gn, nj, halo, g) in groups:
                            js = list(range(nj)) if hh < 6 else list(range(nj - 1, -1, -1))
                            for idx, j in enumerate(js):
                                r0, n, msk, mi = tile_cols(j, g, halo)
                                tiles.append(dict(hh=hh, h=hh % 6, gc0=gc0, gn=gn, halo=halo, j=j, r0=r0, n=n, msk=msk, mi=mi,
                                                  first=(idx == 0), last=(idx == len(js) - 1), gi=gi,
                                                  head_first=(idx == 0 and gc0 == 0)))
                            gi += 1
                    NT = len(tiles)
                    zb = [ps[4], ps[5], ps[6], ps[7]]

                    def build_ncq(hh, par):
                        h = hh
                        ncq = ncqs[hh % 2]
                        for m in range(8):
                            pq = ps[2 * par + (m % 2)]
                            kb.mm(pq[:, 0:128], cown[:, m, h:h + 1].to_broadcast([128, 128]), ident, True, True,
                                  (cown.b, cf.b), (pq.b,))
                            kb.mm(pq[:, 128:130], chal[:, m, h:h + 1].to_broadcast([128, 128]), cf[:, 126:128], True, True,
                                  (chal.b, cf.b), (pq.b,))
                            kb.actf(ncq[:, m * 128:(m + 1) * 128], pq[:, 0:128], AF.Copy, (pq.b,), (ncq.b,), scale=-1.0)
                            kb.actf(ncq[:, 1024 + 2 * m:1026 + 2 * m], pq[:, 128:130], AF.Copy, (pq.b,), (ncq.b,), scale=-1.0)

                    def stA(i):
                        t = tiles[i]
                        hh, h, n = t["hh"], t["h"], t["n"]
                        a0_ = t["gc0"] + t["r0"]
                        kt = ktl[hh % 2]
                        if t["head_first"]:
                            if hh < 6:
                                build_ncq(hh, t["gi"] % 2)
                        Z = zb[i % 4]
                        q_ = qf if hh < 6 else qs
                        kb.mm(Z[:, 0:n], kt[:, t["j"] * 128:(t["j"] + 1) * 128], q_[:, h, a0_:a0_ + n], True, True,
                              (kt.b, q_.b), (Z.b,))
                        if hh >= 6:
                            e_ = ef[i % 2]
                            sp_ = spb[i % 4]
                            kb.actf(e_[:, 0:n], Z[:, 0:n], AF.Exp, (Z.b,), (e_.b,))
                            kb.actf(sp_[:, 0:n], e_[:, 0:n], AF.Ln, (e_.b,), (sp_.b,), bias=1.0)
                            if t["msk"]:
                                if t["halo"]:
                                    kb.tt(pool, sp_[:, 0:16], sp_[:, 0:16], mSh[:, t["mi"], :], ALU.mult, (sp_.b, mb.b), (sp_.b,))
                                else:
                                    kb.tt(pool, sp_[:, 0:128], sp_[:, 0:128], mS[:, t["mi"], :], ALU.mult, (sp_.b, mb.b), (sp_.b,))

                    def stB(i):
                        t = tiles[i]
                        hh, h, n, r0, j = t["hh"], t["h"], t["n"], t["r0"], t["j"]
                        a0_ = t["gc0"] + r0
                        Z = zb[i % 4]
                        par = t["gi"] % 2
                        if hh < 6:
                            ncq = ncqs[hh % 2]
                            s_ = sf[i % 3]
                            p_ = pbf[i % 4]
                            kb.tt(dve, s_[:, 0:n], Z[:, 0:n], ncq[:, a0_:a0_ + n], ALU.add, (Z.b, ncq.b), (s_.b,))
                            if t["msk"]:
                                if t["halo"]:
                                    kb.tt(dve, s_[:, 0:16], s_[:, 0:16], mFh[:, t["mi"], :], ALU.add, (s_.b, mf.b), (s_.b,))
                                else:
                                    kb.tt(dve, s_[:, 0:128], s_[:, 0:128], mF[:, t["mi"], :], ALU.add, (s_.b, mf.b), (s_.b,))
                            kb.actf(p_[:, 0:n], s_[:, 0:n], AF.Exp, (s_.b, ctok.b), (p_.b,), bias=ctok[:, j, h:h + 1])
                        else:
                            sp_ = spb[i % 4]
                            t1 = sf[i % 3]
                            a_ = pbf[i % 4]
                            CS = ps[1] if i % 2 == 0 else ps[3]
                            OT = ps[0] if par == 0 else ps[2]
                            if t["first"]:
                                gn = t["gn"]
                                kb.mm(OT[:, 0:gn], zer_b, qs[:, h, t["gc0"]:t["gc0"] + gn], True, True, (mb.b, qs.b), (OT.b,))
                                kb.mset(pool, ccar[:, 0:gn], 0.0, (ccar.b,))
                            kb.mm(Z[:, 0:n], nti_b, sp_[:, 0:n], False, True, (mb.b, sp_.b), (Z.b,), skip=True)
                            kb.mm(CS[:, 0:n], ones_b, sp_[:, 0:n], True, True, (mb.b, sp_.b), (CS.b,))
                            kb.tt(dve, t1[:, 0:n], Z[:, 0:n], ccar[:, r0:r0 + n], ALU.subtract, (Z.b, ccar.b), (t1.b,))
                            kb.actf(a_[:, 0:n], t1[:, 0:n], AF.Exp, (t1.b,), (a_.b,))
                            if t["msk"]:
                                if t["halo"]:
                                    kb.tt(pool, a_[:, 0:16], a_[:, 0:16], mSh[:, t["mi"], :], ALU.mult, (a_.b, mb.b), (a_.b,))
                                else:
                                    kb.tt(pool, a_[:, 0:128], a_[:, 0:128], mS[:, t["mi"], :], ALU.mult, (a_.b, mb.b), (a_.b,))
                            kb.tt(dve, ccar[:, r0:r0 + n], CS[:, 0:n], ccar[:, r0:r0 + n], ALU.add, (CS.b, ccar.b), (ccar.b,))

                    def stC(i):
                        t = tiles[i]
                        hh, h, n, r0, j = t["hh"], t["h"], t["n"], t["r0"], t["j"]
                        vt = vtl[hh % 2]
                        par = t["gi"] % 2
                        gc0, gn = t["gc0"], t["gn"]
                        p_ = pbf[i % 4]
                        if t["head_first"] and hh + 1 < 12:
                            load_head(hh + 1)
                        if hh < 6:
                            OT = ps[0] if par == 0 else ps[2]
                            DEN = ps[1] if par == 0 else ps[3]
                            kb.mm(OT[:, r0:r0 + n], vt[:, j, :], p_[:, 0:n], t["first"], t["last"], (vt.b, p_.b), (OT.b,))
                            kb.mm(DEN[:, r0:r0 + n], ones_b, p_[:, 0:n], t["first"], t["last"], (mb.b, p_.b), (DEN.b,), inc=True)
                            if t["last"]:
                                kb.op(dve, lambda: nc.vector.reciprocal(rden[:, 0:gn], DEN[:, 0:gn]), (DEN.b,), (rden.b,))
                                kb.tt(dve, oT[:, hh, gc0:gc0 + gn], OT[:, 0:gn], rden[:, 0:gn], ALU.mult, (OT.b, rden.b), (oT.b,))
                        else:
                            OT = ps[0] if par == 0 else ps[2]
                            kb.mm(OT[:, r0:r0 + n], vt[:, j, :], p_[:, 0:n], False, True, (vt.b, p_.b), (OT.b,), skip=True)
                            if t["last"]:
                                kb.actf(oT[:, hh, gc0:gc0 + gn], OT[:, 0:gn], AF.Copy, (OT.b,), (oT.b,))

                    load_head(0)
                    LA, LB = 2, 2
                    NF = sum(1 for t in tiles if t["hh"] < 6)
                    for (lo, hi) in ((0, NF), (NF, NT)):
                        for i in range(lo, hi + LA + LB):
                            if i < hi:
                                stA(i)
                            if lo <= i - LA < hi:
                                stB(i - LA)
                            if lo <= i - LA - LB < hi:
                                stC(i - LA - LB)
                    cnt = [0]
                    for h in range(4):
                        for (c0, c1) in COLT:
                            n = c1 - c0
                            OT = ps[0]
                            DEN = ps[1]
                            for kbk in range(2):
                                ST = zb[cnt[0] % 4]
                                p_ = pbf[cnt[0] % 4]
                                cnt[0] += 1
                                kb.mm(ST[:, 0:n], mkT[:, h, kbk * 128:(kbk + 1) * 128], qm[:, h, c0:c1], True, True,
                                      (mkT.b, qm.b), (ST.b,))
                                kb.actf(p_[:, 0:n], ST[:, 0:n], AF.Exp, (ST.b,), (p_.b,))
                                kb.mm(OT[:, 0:n], mvt[:, kbk, h * 128:(h + 1) * 128], p_[:, 0:n], kbk == 0, kbk == 1,
                                      (mvt.b, p_.b), (OT.b,))
                                kb.mm(DEN[:, 0:n], ones_b, p_[:, 0:n], kbk == 0, kbk == 1, (mb.b, p_.b), (DEN.b,), inc=True)
                            kb.op(dve, lambda: nc.vector.reciprocal(rden[:, 0:n], DEN[:, 0:n]), (DEN.b,), (rden.b,))
                            kb.tt(dve, oT[:, 12 + h, c0:c1], OT[:, 0:n], rden[:, 0:n], ALU.mult, (OT.b, rden.b), (oT.b,))
                    kb.barrier()

            with ExitStack() as mes:
                mg = T(mes, nc, "mg", [128, KC, NQ], BF16)
                NSL = 6
                slots = [T(mes, nc, "ws%d" % i, [128, KC, 256], BF16) for i in range(NSL)]
                ds_sl = [kb.dsem() for _ in range(NSL)]
                sg_t = [T(mes, nc, "sg%d" % i, [128, 512], F32) for i in range(3)]
                tm_t = [T(mes, nc, "tm%d" % i, [128, 512], F32) for i in range(3)]
                xr_t = [T(mes, nc, "xr%d" % i, [128, 512], F32) for i in range(2)]
                ds_xr = [kb.dsem() for _ in range(2)]
                xs_t = [T(mes, nc, "xs%d" % i, [128, 512], F32) for i in range(2)]
                ds_xs = [kb.dsem() for _ in range(2)]
                sqts = [T(mes, nc, "sq%d" % i, [128, 512], BF16) for i in range(2)]
                w_br_v = w_br_d.rearrange("(k p) c -> p k c", p=128)
                w_out_v = w_out_d.rearrange("(k p) c -> p k c", p=128)
                blocks = []
                for fg in range(8):
                    for b in range(3):
                        blocks.append((w_in_v, C_G + b * D + fg * 256))
                    blocks.append((w_br_v, fg * 256))
                for fg in range(8):
                    blocks.append((w_out_v, fg * 256))
                loaded = []

                def ensure(i, depth):
                    while len(loaded) <= min(i + depth, len(blocks) - 1):
                        bi = len(loaded)
                        src, c0 = blocks[bi]
                        sl = slots[bi % NSL]
                        for q4 in range(2):
                            kb.dma(pool, sl[:, q4 * 8:(q4 + 1) * 8, :], src[:, q4 * 8:(q4 + 1) * 8, c0:c0 + 256],
                                   ds_sl[bi % NSL], (), (sl.b,))
                        loaded.append(sl)

                psrr[0] = 0
                for fg in range(8):
                    ensure(fg * 4, 5)
                    gw = [loaded[fg * 4 + b] for b in range(3)]
                    bw = loaded[fg * 4 + 3]
                    for fl in range(2):
                        fc = fg * 2 + fl
                        for (c0, c1) in COLT:
                            n = c1 - c0
                            G = []
                            for b in range(3):
                                acc = psnext()
                                for k in range(KC):
                                    kb.mm(acc[:, 0:n], gw[b][:, k, fl * 128:(fl + 1) * 128], hown[:, k, c0:c1], k == 0, k == KC - 1,
                                          (gw[b].b, hown.b), (acc.b,))
                                G.append(acc)
                            BR = []
                            for b, (h0, h1) in enumerate(((0, 6), (6, 12), (12, 16))):
                                acc = psnext()
                                for hh in range(h0, h1):
                                    kb.mm(acc[:, 0:n], bw[:, hh, fl * 128:(fl + 1) * 128], oT[:, hh, c0:c1], hh == h0, hh == h1 - 1,
                                          (bw.b, oT.b), (acc.b,))
                                BR.append(acc)
                            for b in range(3):
                                kb.actf(sg_t[b][:, 0:n], G[b][:, 0:n], AF.Sigmoid, (G[b].b, vec.b), (sg_t[b].b,),
                                        bias=vcol(V_BG + b * 16 + fc))
                                kb.tt(dve, tm_t[b][:, 0:n], BR[b][:, 0:n], sg_t[b][:, 0:n], ALU.mult, (BR[b].b, sg_t[b].b), (tm_t[b].b,))
                            kb.tt(pool, tm_t[0][:, 0:n], tm_t[0][:, 0:n], tm_t[1][:, 0:n], ALU.add, (tm_t[0].b, tm_t[1].b), (tm_t[0].b,))
                            kb.tt(pool, mg[:, fc, c0:c1], tm_t[0][:, 0:n], tm_t[2][:, 0:n], ALU.add, (tm_t[0].b, tm_t[2].b), (mg.b,))
                kb.barrier()
                ss2 = [ps[5], ps[6], ps[7]]
                psw = [ps[0], ps[1], ps[2], ps[3], ps[4]]
                cnt = [0]
                for fg in range(8):
                    ensure(32 + fg, 4)
                    ww = loaded[32 + fg]
                    for fl in range(2):
                        fc = fg * 2 + fl
                        for ci, (c0, c1) in enumerate(COLT):
                            n = c1 - c0
                            i2 = cnt[0] % 2
                            acc = psw[cnt[0] % 5]
                            cnt[0] += 1
                            xr = xr_t[i2]
                            xs = xs_t[i2]
                            sq = sqts[i2]
                            kb.dma(sp, xr[:, 0:n], xo_v[:, fc, c0:c1], ds_xr[i2], (), (xr.b,))
                            for k in range(KC):
                                kb.mm(acc[:, 0:n], ww[:, k, fl * 128:(fl + 1) * 128], mg[:, k, c0:c1], k == 0, k == KC - 1,
                                      (ww.b, mg.b), (acc.b,))
                            kb.tt(dve, xs[:, 0:n], acc[:, 0:n], xr[:, 0:n], ALU.add, (acc.b, xr.b), (xs.b,))
                            kb.dma(sp, xm_d[fc, :, c0:c1], xs[:, 0:n], ds_xs[i2], (xs.b,), ())
                            kb.tt(pool, sq[:, 0:n], xs[:, 0:n], xs[:, 0:n], ALU.mult, (xs.b,), (sq.b,))
                            kb.mm(ss2[ci][:, 0:n], ones_b, sq[:, 0:n], fc == 0, fc == KC - 1, (sq.b, mb.b), (ss2[ci].b,), inc=True)
                kb.barrier()
                for ci, (c0, c1) in enumerate(COLT):
                    n = c1 - c0
                    kb.actf(rs2[:, c0:c1], ss2[ci][:, 0:n], AF.Ln, (ss2[ci].b,), (rs2.b,), bias=EPS, scale=1.0 / D)
                    kb.actf(rs2[:, c0:c1], rs2[:, c0:c1], AF.Exp, (rs2.b,), (rs2.b,), scale=-0.5)
                kb.barrier()

        with ExitStack() as fes:
            aT = T(fes, nc, "aT", [128, NCH, 1024], BF16)
            with ExitStack() as ues:
                h2 = T(ues, nc, "h2", [128, KC, NQ], BF16)
                with ExitStack() as xes:
                    xl_t = [T(xes, nc, "xl%d" % i, [128, NQ], F32) for i in range(2)]
                    ds_xl = [kb.dsem() for _ in range(2)]
                    for k in range(KC):
                        xl = xl_t[k % 2]
                        kb.dma(sp, xl[:, :], xm_d[k, :, :], ds_xl[k % 2], (), (xl.b,))
                        kb.stt(dve, h2[:, k, :], xl[:, :], vcol(V_GFFN + k), rs2[:, :], ALU.mult, ALU.mult,
                               (xl.b, vec.b, rs2.b), (h2.b,))
                    kb.barrier()
                NSU = 4
                slots = [T(ues, nc, "us%d" % i, [128, KC, 256], BF16) for i in range(NSU)]
                ds_sl = [kb.dsem() for _ in range(NSU)]
                U_t = [T(ues, nc, "U%d" % i, [128, 8, 130], F32) for i in range(2)]
                y_t = [T(ues, nc, "y%d" % i, [128, 8, 128], F32) for i in range(2)]
                sg_t = [T(ues, nc, "sgl%d" % i, [128, 1024], BF16) for i in range(2)]
                w_up_v = w_up_d.rearrange("(k p) c -> p k c", p=128)
                blocks = []
                for cg in range(22):
                    blocks.append(cg * 256)
                    blocks.append(DFF + cg * 256)
                loaded = []

                def ensure_u(i, depth=3):
                    while len(loaded) <= min(i + depth, len(blocks) - 1):
                        bi = len(loaded)
                        sl = slots[bi % NSU]
                        c0 = blocks[bi]
                        for q4 in range(2):
                            kb.dma(pool, sl[:, q4 * 8:(q4 + 1) * 8, :], w_up_v[:, q4 * 8:(q4 + 1) * 8, c0:c0 + 256],
                                   ds_sl[bi % NSU], (), (sl.b,))
                        loaded.append(sl)

                psrr[0] = 0
                ucnt = [0]
                for cg in range(22):
                    ensure_u(cg * 2)
                    for cl in range(2):
                        c = cg * 2 + cl
                        sgl = sg_t[c % 2]
                        for gv in range(2):
                            wsl = loaded[cg * 2 + gv]
                            ch = c + gv * NCH
                            U = U_t[ucnt[0] % 2]
                            y = y_t[ucnt[0] % 2]
                            ucnt[0] += 1
                            for ci, (c0, c1) in enumerate(COLT):
                                n = c1 - c0
                                acc = psnext()
                                for k in range(KC):
                                    kb.mm(acc[:, 0:n], wsl[:, k, cl * 128:(cl + 1) * 128], h2[:, k, c0:c1], k == 0, k == KC - 1,
                                          (wsl.b, h2.b), (acc.b,))
                                if ci < 2:
                                    kb.actf(U[:, ci * 4:(ci + 1) * 4, 2:130], acc[:, :].rearrange("p (m t) -> p m t", m=4),
                                            AF.Copy, (acc.b,), (U.b,))
                                else:
                                    kb.cp(dve, U[:, :, 0:2], acc[:, 0:16].rearrange("p (m t) -> p m t", m=8), (acc.b,), (U.b,))
                            kb.ts(dve, U[:, 0, 0:2], U[:, 0, 0:2], vcol(V_HV), None, ALU.mult, None, (U.b, vec.b), (U.b,))
                            kb.ts(dve, y[:, :, :], U[:, :, 2:130], vcol(V_CW + 2 * 88 + ch), vcol(V_CB + ch), ALU.mult, ALU.add,
                                  (U.b, vec.b), (y.b,))
                            kb.stt(dve, y[:, :, :], U[:, :, 1:129], vcol(V_CW + 1 * 88 + ch), y[:, :, :], ALU.mult, ALU.add,
                                   (U.b, vec.b, y.b), (y.b,))
                            kb.stt(dve, y[:, :, :], U[:, :, 0:128], vcol(V_CW + ch), y[:, :, :], ALU.mult, ALU.add,
                                   (U.b, vec.b, y.b), (y.b,))
                            y2 = y[:, :, :].rearrange("p m t -> p (m t)")
                            if gv == 0:
                                kb.actf(sgl[:, :], y2, AF.Silu, (y.b,), (sgl.b,))
                            else:
                                kb.tt(dve, aT[:, c, :], sgl[:, :], y2, ALU.mult, (sgl.b, y.b), (aT.b,))
                kb.barrier()
            with ExitStack() as des:
                NSD = 6
                slots = [T(des, nc, "dsl%d" % i, [128, 11, 512], BF16) for i in range(NSD)]
                ds_sl = [kb.dsem() for _ in range(NSD)]
                xl_t = [T(des, nc, "xd%d" % i, [128, 512], F32) for i in range(3)]
                ds_xl = [kb.dsem() for _ in range(3)]
                ot_t = [T(des, nc, "ot%d" % i, [128, 512], F32) for i in range(3)]
                ds_ot = [kb.dsem() for _ in range(3)]
                w_dn_v = w_dn_d.rearrange("(c p) f -> p c f", p=128)
                loaded = []

                def ensure_d(i, depth=5):
                    while len(loaded) <= min(i + depth, 15):
                        bi = len(loaded)
                        fg, qt = bi // 4, bi % 4
                        sl = slots[bi % NSD]
                        kb.dma(pool, sl[:, :, :], w_dn_v[:, qt * 11:(qt + 1) * 11, fg * 512:(fg + 1) * 512],
                               ds_sl[bi % NSD], (), (sl.b,))
                        loaded.append(sl)

                psrr[0] = 0
                cnt = [0]
                for fg in range(4):
                    ensure_d(fg * 4)
                    wq4 = [loaded[fg * 4 + q] for q in range(4)]
                    for fl in range(4):
                        fc = fg * 4 + fl
                        for (c0, c1) in COLT[:2]:
                            i3 = cnt[0] % 3
                            cnt[0] += 1
                            xl = xl_t[i3]
                            ot = ot_t[i3]
                            kb.dma(sp, xl[:, :], xm_d[fc, :, c0:c1], ds_xl[i3], (), (xl.b,))
                            acc = psnext()
                            for c in range(NCH):
                                wsl = wq4[c // 11]
                                kb.mm(acc[:, :], wsl[:, c % 11, fl * 128:(fl + 1) * 128], aT[:, c, c0:c1], c == 0, c == NCH - 1,
                                      (wsl.b, aT.b), (acc.b,))
                            kb.tt(dve, ot[:, :], acc[:, :], xl[:, :], ALU.add, (acc.b, xl.b), (ot.b,))
                            kb.dma(sp, out_d[fc * 128:(fc + 1) * 128, c0:c1], ot[:, :], ds_ot[i3], (ot.b,), ())
                kb.barrier()
    return nc


def _host_inputs(inputs):
    f32 = np.float32
    x = np.asarray(inputs["x"], f32)
    mem = np.asarray(inputs["mem"], f32)
    w_in = np.ascontiguousarray(np.asarray(inputs["w_in"], f32)[0])
    w_mkv = np.ascontiguousarray(np.asarray(inputs["w_mem_kv"], f32)[0])
    w_br = np.ascontiguousarray(np.concatenate([np.asarray(inputs["w_br_fox"], f32)[0], np.asarray(inputs["w_br_sb"], f32)[0],
                                                np.asarray(inputs["w_br_mem"], f32)[0]], axis=0))
    w_out = np.ascontiguousarray(np.asarray(inputs["w_out"], f32)[0])
    w_up = np.ascontiguousarray(np.asarray(inputs["w_up"], f32)[0])
    w_dn = np.ascontiguousarray(np.asarray(inputs["w_down"], f32)[0])
    chunkT = lambda v: np.asarray(v, f32).reshape(-1, 128).T
    vec0 = np.zeros((128, NV), f32)
    vec0[:, V_GMIX:V_GMIX + 16] = chunkT(inputs["g_mix"][0])
    vec0[:, V_GFFN:V_GFFN + 16] = chunkT(inputs["g_ffn"][0])
    vec0[:, V_GMEM:V_GMEM + 16] = chunkT(inputs["g_mem"][0])
    vec0[:, V_GQF] = np.asarray(inputs["g_q_fox"], f32)[0]
    vec0[:, V_GKF] = np.asarray(inputs["g_k_fox"], f32)[0]
    vec0[:, V_GQM] = np.asarray(inputs["g_q_mem"], f32)[0]
    vec0[:, V_GKM] = np.asarray(inputs["g_k_mem"], f32)[0]
    bg = np.asarray(inputs["b_gate"], f32)[0]
    for b in range(3):
        vec0[:, V_BG + b * 16:V_BG + (b + 1) * 16] = chunkT(bg[b])
    cw = np.asarray(inputs["conv_w"], f32)[0]
    for k in range(3):
        vec0[:, V_CW + k * 88:V_CW + (k + 1) * 88] = chunkT(cw[k])
    vec0[:, V_CB:V_CB + 88] = chunkT(np.asarray(inputs["conv_b"], f32)[0])
    vec0[:, V_BF:V_BF + 6] = np.asarray(inputs["b_forget"], f32)[0][None, :]
    p = np.arange(128)
    cf = np.zeros((128, 384), f32)
    cf[:, 0:128] = np.eye(128, dtype=f32)
    cf[:, 128:256] = (p[:, None] <= p[None, :]).astype(f32)
    cf[:, 256:384] = 1.0
    in_maps = []
    for c in range(8):
        b, r = c // 4, c % 4
        vec = vec0.copy()
        vec[:, V_SEL + r] = 1.0
        vec[:, V_HV] = 0.0 if r == 0 else 1.0
        pos_main = ((4 * np.arange(8)[:, None] + r) * 128 + p[None, :]).reshape(-1)
        pos_halo = ((4 * np.arange(8)[:, None] + r) * 128 - 2 + np.arange(2)[None, :]).reshape(-1)
        xo = np.zeros((D, NQ), f32)
        xo[:, :1024] = x[b, pos_main, :].T
        valid = pos_halo >= 0
        xo[:, 1024 + np.nonzero(valid)[0]] = x[b, pos_halo[valid], :].T
        xa = np.ascontiguousarray(x[b].T)
        mt = np.ascontiguousarray(mem[b].T)
        mf = np.zeros((128, 1024), f32)
        mb = np.zeros((128, NMB), f32)
        mb[:, MB_ONES:MB_ONES + 128] = 1.0
        mb[:, MB_NTI:MB_NTI + 128] = -(p[:, None] >= p[None, :]).astype(f32)
        for o in range(4):
            if o < r:
                fm = np.zeros((128, 128), f32)
                sm = np.ones((128, 128), f32)
            elif o == r:
                fm = np.where(p[:, None] <= p[None, :], 0.0, NEG).astype(f32)
                sm = (p[:, None] < p[None, :]).astype(f32)
            else:
                fm = np.full((128, 128), NEG, f32)
                sm = np.zeros((128, 128), f32)
            mf[:, o * 128:(o + 1) * 128] = fm
            mb[:, MB_SM + o * 128:MB_SM + (o + 1) * 128] = sm
        for j in range(NB):
            kpos = j * 128 + p
            for i, qp in enumerate(pos_halo):
                if qp >= 0:
                    mf[:, 512 + j * 16 + i] = np.where(kpos <= qp, 0.0, NEG)
                    mb[:, MB_SH + j * 16 + i] = (kpos < qp).astype(f32)
                else:
                    mf[:, 512 + j * 16 + i] = 0.0 if j == 0 else NEG
                    mb[:, MB_SH + j * 16 + i] = 0.0
        in_maps.append({
            "xo": xo, "xa": xa, "mt": mt, "w_in": w_in, "w_mkv": w_mkv, "w_br": w_br, "w_out": w_out,
            "w_up": w_up, "w_dn": w_dn, "vec": vec, "cf": cf, "mf": mf, "mb": mb.astype(ml_dtypes.bfloat16),
        })
    return in_maps


_NC_CACHE = {}


def kernel(**inputs):
    in_maps = _host_inputs(inputs)
    if "nc" not in _NC_CACHE:
        _NC_CACHE["nc"] = build_program()
    nc = _NC_CACHE["nc"]
    res = run_bass_kernel_spmd(nc, in_maps, core_ids=list(range(8)))
    out = np.zeros((2, S, D), np.float32)
    p = np.arange(128)
    for c in range(8):
        b, r = c // 4, c % 4
        pos_main = ((4 * np.arange(8)[:, None] + r) * 128 + p[None, :]).reshape(-1)
        out[b, pos_main, :] = np.asarray(res.results[c]["outT"]).T
    return out
```
